# Optimizing a Trainium2 kernel written in Bass

```python
import math
import jax
import jax.numpy as jnp
from jax import lax
import numpy as np

D_MODEL = 2048
BATCH = 4
SEQ = 4096
DEPTH = 2

RMS_EPS = 1e-6
Q_BLOCK = 128

SSD_HEADS = 32
SSD_HEAD_DIM = 64
SSD_WIDTH = SSD_HEADS * SSD_HEAD_DIM
SSD_GROUPS = 4
SSD_STATE = 128
SSD_CONV = 4
SSD_CHUNK = 128
SSD_CONV_CH = SSD_WIDTH + 2 * SSD_GROUPS * SSD_STATE

MLA_HEADS = 16
MLA_Q_RANK = 512
MLA_KV_RANK = 512
MLA_NOPE = 128
MLA_ROPE = 64
MLA_V = 128
MLA_WIDTH = MLA_HEADS * MLA_V
ROPE_THETA = 10000.0

SB_HEADS = 16
SB_HEAD_DIM = 128
SB_WIDTH = SB_HEADS * SB_HEAD_DIM

FFN_HIDDEN = -(-(8 * D_MODEL) // (3 * 256)) * 256

IN_SPLITS = (SSD_WIDTH, SSD_CONV_CH, SSD_HEADS, MLA_Q_RANK, MLA_KV_RANK, MLA_ROPE)
IN_WIDTH = sum(IN_SPLITS)
IN_OFFSETS = tuple(int(v) for v in np.cumsum(IN_SPLITS)[:-1])
MIX_WIDTH = SSD_WIDTH + MLA_WIDTH

N_EVEN = (DEPTH + 1) // 2
N_ODD = DEPTH // 2

kernel_name = 'hybrid_ssd_mla_stickbreaking_block'


def rms_norm(x, g):
    xf = x.astype(jnp.float32)
    y = xf * lax.rsqrt(jnp.mean(xf * xf, axis=-1, keepdims=True) + RMS_EPS)
    return (y * g.astype(jnp.float32)).astype(x.dtype)


def rope_tables(seq, dtype):
    half = MLA_ROPE // 2
    inv_freq = ROPE_THETA ** (-jnp.arange(half, dtype=jnp.float32) / half)
    ang = jnp.arange(seq, dtype=jnp.float32)[:, None] * inv_freq[None, :]
    return jnp.cos(ang).astype(dtype), jnp.sin(ang).astype(dtype)


def apply_rope(x, cos, sin):
    x1, x2 = jnp.split(x, 2, axis=-1)
    return jnp.concatenate([x1 * cos - x2 * sin, x1 * sin + x2 * cos], axis=-1)


def swiglu_ffn(h, w_gate, w_up, w_down):
    return (jax.nn.silu(h @ w_gate) * (h @ w_up)) @ w_down


def ssd_chunked_scan(x, dt, a, b_in, c_in):
    bsz, seq, nh, hd = x.shape
    g = SSD_GROUPS
    hpg = nh // g
    l = SSD_CHUNK
    nc = seq // l
    xdt = (x * dt[..., None]).reshape(bsz, nc, l, g, hpg, hd)
    adt = (dt * a).reshape(bsz, nc, l, g, hpg).transpose(0, 3, 4, 1, 2)
    a_cs = jnp.cumsum(adt, axis=-1)
    bc = b_in.reshape(bsz, nc, l, g, SSD_STATE)
    cc = c_in.reshape(bsz, nc, l, g, SSD_STATE)
    causal = jnp.tril(jnp.ones((l, l), dtype=bool))
    seg = a_cs[..., :, None] - a_cs[..., None, :]
    decay = jnp.exp(jnp.where(causal, seg, -jnp.inf))
    cb = jnp.einsum('bclgn,bcsgn->bgcls', cc, bc)
    y_diag = jnp.einsum('bghcls,bcsghp->bclghp', cb[:, :, None] * decay, xdt)
    decay_to_end = jnp.exp(a_cs[..., -1:] - a_cs)
    states = jnp.einsum('bclgn,bghcl,bclghp->bcghpn', bc, decay_to_end, xdt)
    chunk_decay = jnp.exp(a_cs[..., -1])

    def step(carry, inp):
        s_c, d_c = inp
        return carry * d_c[..., None, None] + s_c, carry

    init = jnp.zeros(states.shape[:1] + states.shape[2:], states.dtype)
    _, prev = lax.scan(step, init, (jnp.moveaxis(states, 1, 0), jnp.moveaxis(chunk_decay, 3, 0)))
    prev = jnp.moveaxis(prev, 0, 1)
    y_off = jnp.einsum('bclgn,bcghpn,bghcl->bclghp', cc, prev, jnp.exp(a_cs))
    return (y_diag + y_off).reshape(bsz, seq, nh, hd)


def ssd_mixer(z, xbc, dt_raw, conv_w, conv_b, dt_bias, a_log, d_skip, norm_g):
    bsz, seq, _ = xbc.shape
    xbc = lax.conv_general_dilated(
        xbc, conv_w[:, None, :].astype(xbc.dtype), window_strides=(1,),
        padding=[(SSD_CONV - 1, 0)], dimension_numbers=('NWC', 'WIO', 'NWC'),
        feature_group_count=SSD_CONV_CH)
    xbc = jax.nn.silu(xbc + conv_b).astype(jnp.float32)
    x_ssm, b_in, c_in = jnp.split(xbc, [SSD_WIDTH, SSD_WIDTH + SSD_GROUPS * SSD_STATE], axis=-1)
    x_ssm = x_ssm.reshape(bsz, seq, SSD_HEADS, SSD_HEAD_DIM)
    b_in = b_in.reshape(bsz, seq, SSD_GROUPS, SSD_STATE)
    c_in = c_in.reshape(bsz, seq, SSD_GROUPS, SSD_STATE)
    dt = jax.nn.softplus(dt_raw.astype(jnp.float32) + dt_bias.astype(jnp.float32))
    a = -jnp.exp(a_log.astype(jnp.float32))
    y = ssd_chunked_scan(x_ssm, dt, a, b_in, c_in) + d_skip.astype(jnp.float32)[:, None] * x_ssm
    y = y.reshape(bsz, seq, SSD_WIDTH)
    return rms_norm(y * jax.nn.silu(z.astype(jnp.float32)), norm_g).astype(z.dtype)


def mla_mixer(c_q, c_kv, k_rope, q_norm, kv_norm, w_uq, w_ukv, cos, sin):
    bsz, seq, _ = c_q.shape
    q = (rms_norm(c_q, q_norm) @ w_uq).reshape(bsz, seq, MLA_HEADS, MLA_NOPE + MLA_ROPE)
    kv = (rms_norm(c_kv, kv_norm) @ w_ukv).reshape(bsz, seq, MLA_HEADS, MLA_NOPE + MLA_V)
    q_nope, q_pe = jnp.split(q, [MLA_NOPE], axis=-1)
    k_nope, v = jnp.split(kv, [MLA_NOPE], axis=-1)
    q_pe = apply_rope(q_pe, cos[:, None, :], sin[:, None, :])
    k_pe = apply_rope(k_rope, cos, sin)
    scale = (MLA_NOPE + MLA_ROPE) ** -0.5
    outs = []
    for i in range(seq // Q_BLOCK):
        q0, q1 = i * Q_BLOCK, (i + 1) * Q_BLOCK
        s = (jnp.einsum('bqhd,bkhd->bhqk', q_nope[:, q0:q1], k_nope[:, :q1])
             + jnp.einsum('bqhr,bkr->bhqk', q_pe[:, q0:q1], k_pe[:, :q1])).astype(jnp.float32) * scale
        mask = jnp.arange(q1)[None, :] <= jnp.arange(q0, q1)[:, None]
        p = jax.nn.softmax(jnp.where(mask, s, -jnp.inf), axis=-1)
        outs.append(jnp.einsum('bhqk,bkhd->bqhd', p.astype(v.dtype), v[:, :q1]))
    return jnp.concatenate(outs, axis=1).reshape(bsz, seq, MLA_WIDTH)


def ssd_mla_mixer(h, w_in, conv_w, conv_b, dt_bias, a_log, d_skip, ssd_norm,
                  q_norm, kv_norm, w_uq, w_ukv, w_out, cos, sin):
    z, xbc, dt_raw, c_q, c_kv, k_rope = jnp.split(h @ w_in, IN_OFFSETS, axis=-1)
    y_ssd = ssd_mixer(z, xbc, dt_raw, conv_w, conv_b, dt_bias, a_log, d_skip, ssd_norm)
    y_mla = mla_mixer(c_q, c_kv, k_rope, q_norm, kv_norm, w_uq, w_ukv, cos, sin)
    return jnp.concatenate([y_ssd, y_mla], axis=-1) @ w_out


def stick_breaking_mixer(h, w_qkv, w_out):
    bsz, seq, _ = h.shape
    qkv = (h @ w_qkv).reshape(bsz, seq, 3, SB_HEADS, SB_HEAD_DIM)
    q, k, v = qkv[:, :, 0], qkv[:, :, 1], qkv[:, :, 2]
    scale = SB_HEAD_DIM ** -0.5
    outs = []
    for i in range(seq // Q_BLOCK):
        q0, q1 = i * Q_BLOCK, (i + 1) * Q_BLOCK
        z = jnp.einsum('bqhd,bkhd->bhqk', q[:, q0:q1], k[:, :q1]).astype(jnp.float32) * scale
        mask = jnp.arange(q1)[None, :] < jnp.arange(q0, q1)[:, None]
        log_keep = jnp.where(mask, -jax.nn.softplus(z), 0.0)
        later = lax.cumsum(log_keep, axis=log_keep.ndim - 1, reverse=True) - log_keep
        weight = jnp.where(mask, jnp.exp(jax.nn.log_sigmoid(z) + later), 0.0)
        outs.append(jnp.einsum('bhqk,bkhd->bqhd', weight.astype(v.dtype), v[:, :q1]))
    return jnp.concatenate(outs, axis=1).reshape(bsz, seq, SB_WIDTH) @ w_out


def setup_inputs(seed: int = 0) -> dict:
    key = jax.random.key(seed)
    ks = jax.random.split(key, 24)
    f32 = jnp.float32

    def dense(k, shape, fan_in):
        return jax.random.normal(k, shape, f32) * fan_in ** -0.5

    def gain(k, shape):
        return 1.0 + 0.02 * jax.random.normal(k, shape, f32)

    dt0 = jnp.exp(jax.random.uniform(ks[6], (N_EVEN, SSD_HEADS), f32, math.log(1e-3), math.log(1e-1)))
    return {
        'x': jax.random.normal(ks[0], (BATCH, SEQ, D_MODEL), f32),
        'mix_norm': gain(ks[1], (DEPTH, D_MODEL)),
        'ffn_norm': gain(ks[2], (DEPTH, D_MODEL)),
        'w_in': dense(ks[3], (N_EVEN, D_MODEL, IN_WIDTH), D_MODEL),
        'conv_w': dense(ks[4], (N_EVEN, SSD_CONV, SSD_CONV_CH), SSD_CONV),
        'conv_b': 0.02 * jax.random.normal(ks[5], (N_EVEN, SSD_CONV_CH), f32),
        'dt_bias': dt0 + jnp.log(-jnp.expm1(-dt0)),
        'a_log': jnp.log(jax.random.uniform(ks[7], (N_EVEN, SSD_HEADS), f32, 1.0, 16.0)),
        'd_skip': gain(ks[8], (N_EVEN, SSD_HEADS)),
        'ssd_norm': gain(ks[9], (N_EVEN, SSD_WIDTH)),
        'q_norm': gain(ks[10], (N_EVEN, MLA_Q_RANK)),
        'kv_norm': gain(ks[11], (N_EVEN, MLA_KV_RANK)),
        'w_uq': dense(ks[12], (N_EVEN, MLA_Q_RANK, MLA_HEADS * (MLA_NOPE + MLA_ROPE)), MLA_Q_RANK),
        'w_ukv': dense(ks[13], (N_EVEN, MLA_KV_RANK, MLA_HEADS * (MLA_NOPE + MLA_V)), MLA_KV_RANK),
        'w_out_even': dense(ks[14], (N_EVEN, MIX_WIDTH, D_MODEL), MIX_WIDTH),
        'w_qkv': dense(ks[15], (N_ODD, D_MODEL, 3 * SB_WIDTH), D_MODEL),
        'w_out_odd': dense(ks[16], (N_ODD, SB_WIDTH, D_MODEL), SB_WIDTH),
        'w_gate': dense(ks[17], (DEPTH, D_MODEL, FFN_HIDDEN), D_MODEL),
        'w_up': dense(ks[18], (DEPTH, D_MODEL, FFN_HIDDEN), D_MODEL),
        'w_down': dense(ks[19], (DEPTH, FFN_HIDDEN, D_MODEL), FFN_HIDDEN),
        'final_norm': gain(ks[20], (D_MODEL,)),
    }


def reference(x, mix_norm, ffn_norm, w_in, conv_w, conv_b, dt_bias, a_log, d_skip, ssd_norm,
              q_norm, kv_norm, w_uq, w_ukv, w_out_even, w_qkv, w_out_odd,
              w_gate, w_up, w_down, final_norm):
    cos, sin = rope_tables(x.shape[1], x.dtype)
    for layer in range(DEPTH):
        h = rms_norm(x, mix_norm[layer])
        if layer % 2 == 0:
            e = layer // 2
            mix = ssd_mla_mixer(h, w_in[e], conv_w[e], conv_b[e], dt_bias[e], a_log[e], d_skip[e],
                                ssd_norm[e], q_norm[e], kv_norm[e], w_uq[e], w_ukv[e],
                                w_out_even[e], cos, sin)
        else:
            o = layer // 2
            mix = stick_breaking_mixer(h, w_qkv[o], w_out_odd[o])
        x = x + mix
        x = x + swiglu_ffn(rms_norm(x, ffn_norm[layer]), w_gate[layer], w_up[layer], w_down[layer])
    return rms_norm(x, final_norm)
```

```python
import contextlib
import numpy as np
import concourse.bass as bass
import concourse.mybir as mybir
from concourse.bass_utils import run_bass_kernel_spmd

F32 = mybir.dt.float32
BF16 = mybir.dt.bfloat16
AF = mybir.ActivationFunctionType
ALU = mybir.AluOpType
AX = mybir.AxisListType


class Buf:
    __slots__ = ("name", "lw", "rd", "sem", "n", "base", "persist", "dead")

    def __init__(self, name=""):
        self.name = name
        self.lw = None
        self.rd = []
        self.sem = None
        self.n = 0
        self.base = 0
        self.persist = False
        self.dead = False


class Fw:
    CE = ("pe", "act", "dve", "pool")

    def __init__(self, nc, stack):
        self.nc = nc
        self.stack = stack
        self.eng = {"pe": nc.tensor, "act": nc.scalar, "dve": nc.vector, "pool": nc.gpsimd, "sp": nc.sync}
        self.cnt = {e: 0 for e in self.CE}
        self.sems = {e: stack.enter_context(nc.semaphore("s_" + e)) for e in self.CE}
        self.waited = {e: {} for e in self.eng}
        self.nsem = 4
        self.ninst = 0
        self.swq = []
        self.free_sems = []

    def sbuf(self, name, shape, dt):
        return self.stack.enter_context(self.nc.sbuf_tensor(name, shape, dt))

    def psum(self, name, shape, dt):
        return self.stack.enter_context(self.nc.psum_tensor(name, shape, dt))

    def _semobj(self, key):
        if isinstance(key, str):
            return self.sems[key]
        if key.sem is None:
            if self.free_sems:
                key.sem, key.base = self.free_sems.pop()
            else:
                key.sem = self.stack.enter_context(self.nc.semaphore("d%d" % self.nsem))
                key.base = 0
                self.nsem += 1
        return key.sem

    def release(self, key):
        if key.sem is not None:
            self.free_sems.append((key.sem, key.base + 16 * key.n))
            key.sem = None

    def _deps(self, eng, r, w):
        need = {}
        for b in r:
            if b.lw is not None:
                k, v = b.lw
                if need.get(k, 0) < v:
                    need[k] = v
        for b in w:
            if b.lw is not None:
                k, v = b.lw
                if need.get(k, 0) < v:
                    need[k] = v
            for (k, v) in b.rd:
                if need.get(k, 0) < v:
                    need[k] = v
        return need

    def _wait(self, eng, need, skip_key=None):
        E = self.eng[eng]
        wd = self.waited[eng]
        for k, v in need.items():
            if k is skip_key:
                continue
            if not isinstance(k, str) and k.dead:
                continue
            kk = k if isinstance(k, str) else id(k)
            if wd.get(kk, 0) >= v:
                continue
            E.wait_ge(self._semobj(k), v)
            wd[kk] = v
            self.ninst += 1

    def _commit(self, tok, r, w):
        for b in r:
            b.rd.append(tok)
            if len(b.rd) > 64:
                m = {}
                for (k, v) in b.rd:
                    kk = k if isinstance(k, str) else id(k)
                    if kk not in m or m[kk][1] < v:
                        m[kk] = (k, v)
                b.rd = list(m.values())
        for b in w:
            b.lw = tok
            b.rd = []

    def op(self, eng, fn, r=(), w=()):
        need = self._deps(eng, r, w)
        if eng == "pe":
            need.pop("pe", None)
        else:
            pass
        self._wait(eng, need)
        ins = fn(self.eng[eng])
        self.cnt[eng] += 1
        ins.then_inc(self.sems[eng], 1)
        self.ninst += 1
        tok = (eng, self.cnt[eng])
        self._commit(tok, r, w)
        return tok

    def dma(self, q, out, in_, r=(), w=(), key=None, **kw):
        if key is None:
            key = w[0]
        need = self._deps(q, r, w)
        self._wait(q, need, skip_key=key if self._same_key_ok(key, r, w) else None)
        sem = self._semobj(key)
        key.n += 1
        ins = self.eng[q].dma_start(out=out, in_=in_, **kw)
        ins.then_inc(sem, 16)
        self.ninst += 1
        tok = (key, key.base + 16 * key.n)
        self._commit(tok, r, w)
        return tok

    def dmalike(self, q, fn, r=(), w=(), key=None):
        need = self._deps(q, r, w)
        self._wait(q, need)
        sem = self._semobj(key)
        key.n += 1
        ins = fn(self.eng[q])
        ins.then_inc(sem, 16)
        self.ninst += 1
        tok = (key, key.base + 16 * key.n)
        self._commit(tok, r, w)
        return tok

    def _same_key_ok(self, key, r, w):
        for b in r:
            if b.lw is not None and b.lw[0] is key:
                return False
        for b in w:
            for (k, v) in b.rd:
                if k is key:
                    return False
        return True

    def barrier_wait(self, eng, toks):
        need = {}
        for (k, v) in toks:
            if need.get(k, 0) < v:
                need[k] = v
        self._wait(eng, need)


D = 2048
FH = 5632
EPS = 1e-6
TT = 512
NSUB = TT // 128


class T:
    __slots__ = ("t", "b")

    def __init__(self, t, b):
        self.t = t
        self.b = b


class Builder:
    def __init__(self, S, inputs_decl):
        self.S = S
        self.nc = nc = bass.Bass("TRN2", target_bir_lowering=False)
        self.root = contextlib.ExitStack()
        self.fw = Fw(nc, self.root)
        self.inp = {}
        for name, shape in inputs_decl.items():
            self.inp[name] = nc.dram_tensor(name, list(shape), F32, kind="ExternalInput").ap()
        self.dbufs = {}
        self.wts = {}
        self.scr = {}
        self.keys = []
        self.dead = []
        self.persist = True
        self.wslot_i = 0
        self.cast_q = []
        self.cast_keys = []
        self.cast_i = 0
        self.uid = 0
        self.marks = []

    def db(self, name, idx=0):
        k = (name, idx)
        b = self.dbufs.get(k)
        if b is None:
            b = self.dbufs[k] = Buf("%s_%s" % (name, idx))
        return b

    def key(self, name):
        b = Buf(name)
        b.persist = self.persist
        self.keys.append(b)
        return b

    def tile(self, st, name, shape, dt, psum=False):
        t = st.enter_context((self.nc.psum_tensor if psum else self.nc.sbuf_tensor)(self.nm(name), list(shape), dt))
        return T(t, self.key(name))

    def nm(self, name):
        self.uid += 1
        return "%s_u%d" % (name, self.uid)

    def dram(self, name, shape, dt, kind="Internal"):
        ap = self.nc.dram_tensor(name, list(shape), dt, kind=kind).ap()
        self.scr[name] = ap
        return ap

    def barrier(self):
        fw = self.fw
        toks = [(e, fw.cnt[e]) for e in fw.CE if fw.cnt[e] > 0]
        toks += [(b, b.base + 16 * b.n) for b in self.keys if b.n > 0 and b.sem is not None]
        for e in ("pe", "act", "dve", "pool", "sp"):
            fw.barrier_wait(e, toks)
        self.marks.append(dict(fw.cnt))
        dead = [b for b in self.keys if not b.persist]
        for b in dead:
            fw.release(b)
            b.dead = True
        self.dead.extend(dead)
        self.keys = [b for b in self.keys if b.persist]

    def prep_w(self, name, W, cw=512, kcg_max=16, defer=False):
        K, N = W.shape
        KC = K // 128
        KG = (KC + kcg_max - 1) // kcg_max
        kcg = KC // KG
        assert KG * kcg == KC and N % cw == 0
        NT = N // cw
        ws = self.dram("ws_" + name, [NT, KG, 128, kcg, cw], BF16)
        src_b = self.db("in_" + name)
        tiles = {}
        jobs = []
        for ct in range(NT):
            for kg in range(KG):
                src = W[kg * kcg * 128:(kg + 1) * kcg * 128, ct * cw:(ct + 1) * cw].rearrange("(kc p) n -> p kc n", p=128)
                tb = Buf("w_%s_%d_%d" % (name, ct, kg))
                tiles[(ct, kg)] = tb
                jobs.append(lambda dst=ws[ct, kg], src=src, tb=tb: self.cast_dma(dst, src, src_b, tb))
        self.wts[name] = (ws, tiles, kcg, KG, NT)
        if defer:
            self.cast_q.extend(jobs)
        else:
            for j in jobs:
                j()

    def cast_dma(self, dst, src, src_b, tb, same_key=None):
        if not self.cast_keys:
            old_p, self.persist = self.persist, True
            self.cast_keys = [self.key("castk%d" % i) for i in range(4)]
            self.persist = old_p
        key = same_key if same_key is not None else self.cast_keys[self.cast_i % 4]
        if same_key is None:
            self.cast_i += 1
        if key.n > 0:
            self.fw._wait("pool", {key: key.base + 16 * key.n})
        self.fw.dma("pool", dst, src, r=[src_b], w=[tb], key=key)
        return key

    def pump(self, n=1):
        while n > 0 and self.cast_q:
            self.cast_q.pop(0)()
            n -= 1

    def flush_casts(self):
        self.pump(len(self.cast_q))

    def wload(self, name, ct, kg=0):
        ws, tiles, kcg, KG, NT = self.wts[name]
        slot = self.wring[self.wslot_i % len(self.wring)]
        self.wslot_i += 1
        cw = ws.shape[-1]
        view = slot.t[:, 0:kcg * cw].rearrange("p (k n) -> p k n", n=cw)
        self.fw.dma("sp", view, ws[ct, kg], r=[tiles[(ct, kg)]], w=[slot.b], key=slot.b)
        return T(view, slot.b)

    def setup_consts(self, st):
        fw = self.fw
        self.identf = self.tile(st, "identf", [128, 128], F32)
        self.ident = self.tile(st, "ident", [128, 128], BF16)
        fw.op("pool", lambda E: E.memset(self.identf.t[:], 1.0), w=[self.identf.b])
        fw.op("pool", lambda E: E.affine_select(out=self.identf.t[:], in_=self.identf.t[:], pattern=[[1, 128]],
                                                compare_op=ALU.is_equal, fill=0.0, base=0, channel_multiplier=-1),
              r=[self.identf.b], w=[self.identf.b])
        fw.op("dve", lambda E: E.tensor_copy(out=self.ident.t[:], in_=self.identf.t[:]), r=[self.identf.b], w=[self.ident.b])
        self.mhalf = self.tile(st, "mhalf", [128, 1], F32)
        fw.op("pool", lambda E: E.memset(self.mhalf.t[:], -0.5), w=[self.mhalf.b])
        self.wring = [self.tile(st, "wring%d" % i, [128, 16 * 512], BF16) for i in range(4)]
        self.persist = False

    def bcast_row(self, st, name, row_ap, n):
        t = self.tile(st, name, [128, n], F32)
        self.fw.dma("sp", t.t[:], row_ap.partition_broadcast(128), r=[self.db("in_" + name)], w=[t.b])
        return t

    def rmsnorm(self, xin, xb, Dn, gb, out, outb, junk, stat):
        fw = self.fw
        ss, ms, rstd = stat
        fw.op("dve", lambda E: E.scalar_tensor_tensor(out=junk.t[:, 0:Dn], in0=xin, scalar=1.0, in1=xin,
                                                      op0=ALU.mult, op1=ALU.mult, accum_out=ss.t[:]),
              r=[xb], w=[junk.b, ss.b])
        fw.op("dve", lambda E: E.tensor_scalar(out=ms.t[:], in0=ss.t[:], scalar1=1.0 / Dn, scalar2=EPS,
                                               op0=ALU.mult, op1=ALU.add), r=[ss.b], w=[ms.b])
        fw.op("act", lambda E: E.activation(out=ms.t[:], in_=ms.t[:], func=AF.Sqrt), r=[ms.b], w=[ms.b])
        fw.op("dve", lambda E: E.reciprocal(out=rstd.t[:], in_=ms.t[:]), r=[ms.b], w=[rstd.b])
        fw.op("dve", lambda E: E.scalar_tensor_tensor(out=out, in0=xin, scalar=rstd.t[:], in1=gb.t[:, 0:Dn],
                                                      op0=ALU.mult, op1=ALU.mult),
              r=[xb, rstd.b, gb.b], w=[outb])

    def transpose_to(self, src, srcb, nch, dst, dstb_list, ch0, col0, pts, flip=0):
        fw = self.fw
        if not isinstance(pts, (list, tuple)):
            pts = [pts]
        for g0 in range(0, nch, 8):
            g = min(8, nch - g0)
            pt = pts[(g0 // 8 + flip) % len(pts)]
            for j in range(g):
                fw.op("pe", lambda E: E.transpose(out=pt.t[:, j, :], in_=src[:, (g0 + j) * 128:(g0 + j + 1) * 128],
                                                  identity=self.ident.t[:]),
                      r=[srcb, self.ident.b], w=[pt.b])
            eng = "act" if ((g0 // 8 + flip) % 2 == 0) else "dve"
            o = dst[:, ch0 + g0:ch0 + g0 + g, col0:col0 + 128]
            i = pt.t[:, 0:g, :]
            if eng == "act":
                fw.op("act", lambda E: E.copy(out=o, in_=i), r=[pt.b], w=[dstb_list[g0 // 8]])
            else:
                fw.op("dve", lambda E: E.tensor_copy(out=o, in_=i), r=[pt.b], w=[dstb_list[g0 // 8]])

    def ffn(self, L, xin, xin_name, xout, xout_name, gain_row):
        fw = self.fw
        S = self.S
        wg, wu, wd = "wg%d" % L, "wu%d" % L, "wd%d" % L
        with contextlib.ExitStack() as st:
            gb = self.bcast_row(st, "ffn_g%d" % L, gain_row, D)
            xs = [self.tile(st, "f_xs%d" % i, [128, D], F32) for i in range(2)]
            hb = [self.tile(st, "f_hb%d" % i, [128, D], BF16) for i in range(2)]
            junk = self.tile(st, "f_junk", [128, D], BF16)
            stats = [[self.tile(st, "f_st%d_%d" % (i, j), [128, 1], F32) for j in range(3)] for i in range(2)]
            hTs = [st.enter_context(self.nc.sbuf_tensor(self.nm("f_hT%d" % i), [128, 16, TT], BF16)) for i in range(2)]
            hTbs = [[[self.key("f_hTb%d_%d_%d" % (i, s, h)) for h in range(2)] for s in range(NSUB)] for i in range(2)]
            aT = st.enter_context(self.nc.sbuf_tensor(self.nm("f_aT"), [128, FH // 128, TT], BF16))
            aTb = [self.key("f_aTb%d" % i) for i in range(FH // 128)]
            sg = [self.tile(st, "f_sg%d" % i, [128, TT], F32) for i in range(2)]
            xr = [self.tile(st, "f_xr%d" % i, [128, 512], F32) for i in range(4)]
            xo = [self.tile(st, "f_xo%d" % i, [128, 512], F32) for i in range(4)]
            pt = [self.tile(st, "f_pt%d" % i, [128, 8, 128], BF16, psum=True) for i in range(2)]
            pg = self.tile(st, "f_pg", [128, TT], F32, psum=True)
            pu = self.tile(st, "f_pu", [128, TT], F32, psum=True)
            py = [self.tile(st, "f_py%d" % i, [128, 512], F32, psum=True) for i in range(NSUB)]
            nfc = FH // 128
            def norm_tile(tt):
                for s in range(NSUB):
                    row = tt * NSUB + s
                    x_ = xs[row % 2]
                    h_ = hb[row % 2]
                    fw.dma("sp", x_.t[:], xin[row * 128:(row + 1) * 128, :], r=[self.db(xin_name, row)], w=[x_.b])
                    self.rmsnorm(x_.t[:], x_.b, D, gb, h_.t[:], h_.b, junk, stats[row % 2])
                    self.transpose_to(h_.t, h_.b, 16, hTs[tt % 2], hTbs[tt % 2][s], 0, s * 128, pt, flip=s)

            norm_tile(0)
            for tt in range(S // TT):
                hT = hTs[tt % 2]
                hTb = hTbs[tt % 2]
                hT_all = [b for l in hTb for b in l]
                for ft in range(FH // 512):
                    wgt = self.wload(wg, ft)
                    wut = self.wload(wu, ft)
                    for j in range(4):
                        fc = ft * 4 + j
                        for kc in range(16):
                            fw.op("pe", lambda E: E.matmul(pg.t[:], lhsT=wgt.t[:, kc, j * 128:(j + 1) * 128], rhs=hT[:, kc, :],
                                                           start=(kc == 0), stop=(kc == 15)),
                                  r=[wgt.b] + hT_all, w=[pg.b])
                        for kc in range(16):
                            fw.op("pe", lambda E: E.matmul(pu.t[:], lhsT=wut.t[:, kc, j * 128:(j + 1) * 128], rhs=hT[:, kc, :],
                                                           start=(kc == 0), stop=(kc == 15)),
                                  r=[wut.b] + hT_all, w=[pu.b])
                        sg_ = sg[fc % 2]
                        fw.op("act", lambda E: E.activation(out=sg_.t[:], in_=pg.t[:], func=AF.Silu), r=[pg.b], w=[sg_.b])
                        fw.op("dve", lambda E: E.tensor_tensor(out=aT[:, fc, :], in0=sg_.t[:], in1=pu.t[:], op=ALU.mult),
                              r=[sg_.b, pu.b], w=[aTb[fc]])
                if tt + 1 < S // TT:
                    norm_tile(tt + 1)
                for dt in range(D // 512):
                    for kg in range(4):
                        wdt = self.wload(wd, dt, kg)
                        for s in range(NSUB):
                            for k in range(11):
                                fc = kg * 11 + k
                                fw.op("pe", lambda E: E.matmul(py[s].t[:], lhsT=aT[:, fc, s * 128:(s + 1) * 128], rhs=wdt.t[:, k, :],
                                                               start=(fc == 0), stop=(fc == nfc - 1)),
                                      r=[wdt.b, aTb[fc]], w=[py[s].b])
                    for s in range(NSUB):
                        row = tt * NSUB + s
                        fw.dma("sp", xr[s].t[:], xin[row * 128:(row + 1) * 128, dt * 512:(dt + 1) * 512],
                               r=[self.db(xin_name, row)], w=[xr[s].b])
                        fw.op("dve", lambda E: E.tensor_tensor(out=xo[s].t[:], in0=py[s].t[:], in1=xr[s].t[:], op=ALU.add),
                              r=[py[s].b, xr[s].b], w=[xo[s].b])
                        fw.dma("act", xout[row * 128:(row + 1) * 128, dt * 512:(dt + 1) * 512], xo[s].t[:],
                               r=[xo[s].b], w=[self.db(xout_name, row)], key=xo[s].b)
            self.barrier()

    def final_norm(self, xin, xin_name, xout, xout_name, gain_row):
        fw = self.fw
        with contextlib.ExitStack() as st:
            gb = self.bcast_row(st, "fin_g", gain_row, D)
            xs = [self.tile(st, "n_xs%d" % i, [128, D], F32) for i in range(3)]
            os_ = [self.tile(st, "n_os%d" % i, [128, D], F32) for i in range(3)]
            junk = self.tile(st, "n_junk", [128, D], BF16)
            stats = [[self.tile(st, "n_st%d_%d" % (i, j), [128, 1], F32) for j in range(3)] for i in range(3)]
            toks = []
            for row in range(self.S // 128):
                x_ = xs[row % 3]
                o_ = os_[row % 3]
                fw.dma("sp", x_.t[:], xin[row * 128:(row + 1) * 128, :], r=[self.db(xin_name, row)], w=[x_.b])
                self.rmsnorm(x_.t[:], x_.b, D, gb, o_.t[:], o_.b, junk, stats[row % 3])
                toks.append(fw.dma("act", xout[row * 128:(row + 1) * 128, :], o_.t[:], r=[o_.b], w=[self.db(xout_name, row)], key=o_.b))
            self.barrier()

    def norm_T(self, st_tiles, xin, xin_name, tt, gb, hT, hTb, pt):
        xs, hb, junk, stats = st_tiles
        fw = self.fw
        for s in range(NSUB):
            row = tt * NSUB + s
            x_ = xs[row % 2]
            h_ = hb[row % 2]
            fw.dma("sp", x_.t[:], xin[row * 128:(row + 1) * 128, :], r=[self.db(xin_name, row)], w=[x_.b])
            self.rmsnorm(x_.t[:], x_.b, D, gb, h_.t[:], h_.b, junk, stats[row % 2])
            self.transpose_to(h_.t, h_.b, 16, hT, hTb[s], 0, s * 128, pt, flip=s)

    def norm_tiles(self, st, pfx):
        xs = [self.tile(st, pfx + "_xs%d" % i, [128, D], F32) for i in range(2)]
        hb = [self.tile(st, pfx + "_hb%d" % i, [128, D], BF16) for i in range(2)]
        junk = self.tile(st, pfx + "_junk", [128, D], BF16)
        stats = [[self.tile(st, pfx + "_st%d_%d" % (i, j), [128, 1], F32) for j in range(3)] for i in range(2)]
        return xs, hb, junk, stats

    def proj_resid(self, tt, aT, aTb, wname, KG, kcg, py, xr, xo, xin, xin_name, xout, xout_name):
        fw = self.fw
        nkc = KG * kcg
        for dt in range(D // 512):
            for kg in range(KG):
                wdt = self.wload(wname, dt, kg)
                for s in range(NSUB):
                    for k in range(kcg):
                        fc = kg * kcg + k
                        fw.op("pe", lambda E: E.matmul(py[s].t[:], lhsT=aT[:, fc, s * 128:(s + 1) * 128], rhs=wdt.t[:, k, :],
                                                       start=(fc == 0), stop=(fc == nkc - 1)),
                              r=[wdt.b, aTb[fc]], w=[py[s].b])
            for s in range(NSUB):
                row = tt * NSUB + s
                fw.dma("sp", xr[s].t[:], xin[row * 128:(row + 1) * 128, dt * 512:(dt + 1) * 512],
                       r=[self.db(xin_name, row)], w=[xr[s].b])
                fw.op("dve", lambda E: E.tensor_tensor(out=xo[s].t[:], in0=py[s].t[:], in1=xr[s].t[:], op=ALU.add),
                      r=[py[s].b, xr[s].b], w=[xo[s].b])
                fw.dma("act", xout[row * 128:(row + 1) * 128, dt * 512:(dt + 1) * 512], xo[s].t[:],
                       r=[xo[s].b], w=[self.db(xout_name, row)], key=xo[s].b)

    def out_proj(self, wname, KC, ymT, ym_name, xin, xin_name, xout, xout_name):
        fw = self.fw
        KG = max(1, KC // 16)
        kcg = KC // KG
        with contextlib.ExitStack() as st:
            yT = [st.enter_context(self.nc.sbuf_tensor(self.nm("o_yT%d" % i), [128, KC, TT], BF16)) for i in range(2)]
            yTb = [[self.key("o_yTb%d_%d" % (i, k)) for k in range(KC)] for i in range(2)]
            xr = [self.tile(st, "o_xr%d" % i, [128, 512], F32) for i in range(4)]
            xo = [self.tile(st, "o_xo%d" % i, [128, 512], F32) for i in range(4)]
            py = [self.tile(st, "o_py%d" % i, [128, 512], F32, psum=True) for i in range(NSUB)]
            for tt in range(self.S // TT):
                y_ = yT[tt % 2]
                yb_ = yTb[tt % 2]
                for k in range(KC):
                    fw.dma("sp", y_[:, k, :], ymT[k * 128:(k + 1) * 128, tt * TT:(tt + 1) * TT],
                           r=[self.db(ym_name, (k, tt))], w=[yb_[k]])
                self.proj_resid(tt, y_, yb_, wname, KG, kcg, py, xr, xo, xin, xin_name, xout, xout_name)
            self.barrier()

    def sb_qkv(self, xin, xin_name, gain_row, qT, kT, vv, qscale):
        fw = self.fw
        with contextlib.ExitStack() as st:
            gb = self.bcast_row(st, "sbq_g", gain_row, D)
            nt = self.norm_tiles(st, "sbq")
            hTs = [st.enter_context(self.nc.sbuf_tensor(self.nm("sbq_hT%d" % i), [128, 16, TT], BF16)) for i in range(2)]
            hTbs = [[[self.key("sbq_hTb%d_%d_%d" % (i, s, h)) for h in range(2)] for s in range(NSUB)] for i in range(2)]
            ev = [self.tile(st, "sbq_ev%d" % i, [128, 512], BF16) for i in range(8)]
            pt = [self.tile(st, "sbq_pt%d" % i, [128, 8, 128], BF16, psum=True) for i in range(2)]
            pp = [self.tile(st, "sbq_pp%d" % i, [128, 512], F32, psum=True) for i in range(4)]
            n = 0
            self.norm_T(nt, xin, xin_name, 0, gb, hTs[0], hTbs[0], pt)
            for tt in range(self.S // TT):
                hT = hTs[tt % 2]
                hTb = hTbs[tt % 2]
                hT_all = [b for l in hTb for b in l]
                for ct in range(12):
                    if ct == 8 and tt + 1 < self.S // TT:
                        self.norm_T(nt, xin, xin_name, tt + 1, gb, hTs[(tt + 1) % 2], hTbs[(tt + 1) % 2], pt)
                    wt = self.wload("wqkv", ct)
                    for j in range(4):
                        p_ = pp[n % 4]
                        e_ = ev[n % 8]
                        if ct < 8:
                            for kc in range(16):
                                fw.op("pe", lambda E: E.matmul(p_.t[:], lhsT=wt.t[:, kc, j * 128:(j + 1) * 128], rhs=hT[:, kc, :],
                                                               start=(kc == 0), stop=(kc == 15)), r=[wt.b] + hT_all, w=[p_.b])
                            h = (ct % 4) * 4 + j
                            dst = (qT if ct < 4 else kT)[h, :, tt * TT:(tt + 1) * TT]
                            dname = ("qT" if ct < 4 else "kT", (h, tt))
                        else:
                            for kc in range(16):
                                fw.op("pe", lambda E: E.matmul(p_.t[:], lhsT=hT[:, kc, j * 128:(j + 1) * 128], rhs=wt.t[:, kc, :],
                                                               start=(kc == 0), stop=(kc == 15)), r=[wt.b] + hT_all, w=[p_.b])
                            row = tt * NSUB + j
                            h0 = (ct - 8) * 4
                            dst = vv[h0:h0 + 4, row * 128:(row + 1) * 128, :].rearrange("h t d -> t h d")
                            dname = ("vv", (ct - 8, row))
                        if ct < 4:
                            if n % 2 == 0:
                                fw.op("act", lambda E: E.mul(out=e_.t[:], in_=p_.t[:], mul=qscale), r=[p_.b], w=[e_.b])
                            else:
                                fw.op("dve", lambda E: E.tensor_scalar(out=e_.t[:], in0=p_.t[:], scalar1=qscale, scalar2=None, op0=ALU.mult),
                                      r=[p_.b], w=[e_.b])
                        elif n % 2 == 0:
                            fw.op("act", lambda E: E.copy(out=e_.t[:], in_=p_.t[:]), r=[p_.b], w=[e_.b])
                        else:
                            fw.op("dve", lambda E: E.tensor_copy(out=e_.t[:], in_=p_.t[:]), r=[p_.b], w=[e_.b])
                        src = e_.t[:] if ct < 8 else e_.t[:].rearrange("t (h d) -> t h d", h=4)
                        fw.dma("act", dst, src, r=[e_.b], w=[self.db(*dname)], key=e_.b)
                        n += 1
            self.barrier()

    def attn_consts(self, st):
        fw = self.fw
        c = {}

        def mk(name, val, pattern, cm, cmp_):
            f = self.tile(st, "ac_f_" + name, [128, 128], F32)
            b = self.tile(st, "ac_b_" + name, [128, 128], BF16)
            fw.op("pool", lambda E: E.memset(f.t[:], val), w=[f.b])
            fw.op("pool", lambda E: E.affine_select(out=f.t[:], in_=f.t[:], pattern=pattern, compare_op=cmp_, fill=0.0,
                                                    base=0, channel_multiplier=cm), r=[f.b], w=[f.b])
            fw.op("dve", lambda E: E.tensor_copy(out=b.t[:], in_=f.t[:]), r=[f.b], w=[b.b])
            c[name] = (f, b)

        mk("triIN", -1.0, [[-1, 128]], 1, ALU.is_ge)
        mk("triSN", -1.0, [[1, 128]], -1, ALU.is_gt)
        mk("mS", 1.0, [[1, 128]], -1, ALU.is_gt)
        mk("mC", 1.0, [[1, 128]], -1, ALU.is_ge)
        ones = self.tile(st, "ac_ones", [128, 128], BF16)
        zeros = self.tile(st, "ac_zeros", [128, 512], BF16)
        fw.op("pool", lambda E: E.memset(ones.t[:], 1.0), w=[ones.b])
        fw.op("pool", lambda E: E.memset(zeros.t[:], 0.0), w=[zeros.b])
        c["ones"] = ones
        c["zeros"] = zeros
        return c

    def attention(self, kind, nheads, qT, kT, vv, ymT, ym_name, row0, scale, qpT=None, kpT=None):
        fw = self.fw
        S = self.S
        NQT = S // TT
        NCH = S // 128
        NS = 4 if kind == "mla" else 2
        NKV = 1 if kind == "mla" else 2
        NPS = 1 if kind == "mla" else 2
        NB = 3
        LEAD = 1 if kind == "mla" else 2
        with contextlib.ExitStack() as st:
            C = self.attn_consts(st)
            onesf = self.tile(st, "at_onesf", [128, 128], F32)
            fw.op("pool", lambda E: E.memset(onesf.t[:], 1.0), w=[onesf.b])
            kpe = None
            if kind == "mla":
                kpe = self.tile(st, "at_kpe", [128, S], BF16)
                fw.op("pool", lambda E: E.memset(kpe.t[64:128, :], 0.0), w=[kpe.b])
                fw.dma("sp", kpe.t[0:64, :], kpT, r=[self.db("kpT")], w=[kpe.b])
            streams = []
            for si in range(NS):
                R = {}
                R["k"] = [self.tile(st, "at_k%d_%d" % (si, i), [128, S], BF16) for i in range(NKV)]
                R["v"] = [self.tile(st, "at_v%d_%d" % (si, i), [128, NCH, 128], BF16) for i in range(NKV)]
                R["q"] = [self.tile(st, "at_q%d_%d" % (si, i), [128, TT], BF16) for i in range(2)]
                if kind == "mla":
                    R["qp"] = [self.tile(st, "at_qp%d_%d" % (si, i), [128, TT], BF16) for i in range(2)]
                    for qp__ in R["qp"]:
                        fw.op("pool", lambda E: E.memset(qp__.t[64:128, :], 0.0), w=[qp__.b])
                    R["P"] = [self.tile(st, "at_P%d_%d" % (si, i), [128, TT], BF16) for i in range(3)]
                    R["acc"] = self.tile(st, "at_acc%d" % si, [128, TT], F32)
                    R["rs"] = self.tile(st, "at_rs%d" % si, [128, TT], F32)
                else:
                    for nm, dt_ in (("e", F32), ("sp", F32), ("hi", BF16), ("lo", BF16), ("P", BF16)):
                        R[nm] = [self.tile(st, "at_%s%d_%d" % (nm, si, i), [128, TT], dt_) for i in range(NB if nm in ("e", "sp", "hi", "lo") else 2)]
                    R["kn"] = [self.tile(st, "at_kn%d_%d" % (si, i), [128, S], BF16) for i in range(1)]
                R["o"] = [self.tile(st, "at_o%d_%d" % (si, i), [128, TT], BF16) for i in range(2)]
                R["ps"] = [self.tile(st, "at_ps%d_%d" % (si, i), [128, TT], F32, psum=True) for i in range(NPS)]
                R["A"] = self.tile(st, "at_A%d" % si, [128, TT], F32, psum=True) if kind == "sb" else R["ps"][0]
                R["O"] = self.tile(st, "at_O%d" % si, [128, TT], F32, psum=True)
                streams.append(R)

            def geom(c, qt):
                j = c - 4 * qt
                q0 = j * 128 if j > 0 else 0
                return q0, (j >= 0)

            def stream(si):
                R = streams[si]
                hn = 0
                cnt = 0
                for h in range(si, nheads, NS):
                    k_ = R["k"][hn % NKV]
                    v_ = R["v"][hn % NKV]
                    hn += 1
                    fw.dma("sp", k_.t[:], kT[h], r=[self.db("kT_all")], w=[k_.b])
                    fw.dma("sp", v_.t[:], vv[h].rearrange("(c p) d -> p c d", p=128), r=[self.db("vv_all")], w=[v_.b])
                    kn_ = None
                    if kind == "sb":
                        kn_ = R["kn"][0]
                        fw.op("pool", lambda E: E.tensor_scalar(out=kn_.t[:], in0=k_.t[:], scalar1=-1.0, scalar2=0.0, op0=ALU.mult, op1=ALU.add),
                              r=[k_.b], w=[kn_.b])
                    for qt in range(NQT):
                        self.pump(1)
                        q_ = R["q"][qt % 2]
                        fw.dma("sp", q_.t[:], qT[h, :, qt * TT:(qt + 1) * TT], r=[self.db("qT_all")], w=[q_.b])
                        qp_ = None
                        if kind == "mla":
                            qp_ = R["qp"][qt % 2]
                            fw.dma("sp", qp_.t[0:64, :], qpT[h, :, qt * TT:(qt + 1) * TT], r=[self.db("qpT_all")], w=[qp_.b])
                        A = R["A"]
                        O = R["O"]
                        fw.op("pe", lambda E: E.matmul(O.t[:], lhsT=C["zeros"].t[:, 0:128], rhs=C["zeros"].t[:], start=True, stop=False),
                              r=[C["zeros"].b], w=[O.b])
                        if kind == "sb":
                            fw.op("pe", lambda E: E.matmul(A.t[:], lhsT=C["zeros"].t[:, 0:128], rhs=C["zeros"].t[:], start=True, stop=False),
                                  r=[C["zeros"].b], w=[A.b])
                        else:
                            acc = R["acc"]
                            fw.op("pool", lambda E: E.memset(acc.t[:], 0.0), w=[acc.b])
                        nk = 4 * (qt + 1)
                        order = list(range(nk - 1, -1, -1))

                        def stageA(c, slot):
                            q0, diag = geom(c, qt)
                            ps = R["ps"][slot % NPS]
                            kc_ap = k_.t[:, c * 128:(c + 1) * 128]
                            if kind == "mla":
                                P = R["P"][slot % 3]
                                fw.op("pe", lambda E: E.matmul(ps.t[:, q0:], lhsT=kc_ap, rhs=q_.t[:, q0:], start=True, stop=False),
                                      r=[k_.b, q_.b], w=[ps.b])
                                fw.op("pe", lambda E: E.matmul(ps.t[:, q0:], lhsT=kpe.t[:, c * 128:(c + 1) * 128], rhs=qp_.t[:, q0:],
                                                               start=False, stop=True), r=[kpe.b, qp_.b], w=[ps.b])
                                fw.op("act", lambda E: E.activation(out=P.t[:, q0:], in_=ps.t[:, q0:], func=AF.Exp, scale=scale),
                                      r=[ps.b], w=[P.b])
                                if diag:
                                    fw.op("pool", lambda E: E.tensor_tensor(out=P.t[:, q0:q0 + 128], in0=P.t[:, q0:q0 + 128],
                                                                            in1=C["mC"][1].t[:], op=ALU.mult),
                                          r=[P.b, C["mC"][1].b], w=[P.b])
                                fw.op("dve", lambda E: E.tensor_tensor(out=acc.t[:, q0:], in0=acc.t[:, q0:], in1=P.t[:, q0:], op=ALU.add),
                                      r=[acc.b, P.b], w=[acc.b])
                                return
                            i3 = slot % NB
                            e_, sp_, hi_, lo_ = (R[n_][i3] for n_ in ("e", "sp", "hi", "lo"))
                            fw.op("pe", lambda E: E.matmul(ps.t[:, q0:], lhsT=kc_ap, rhs=q_.t[:, q0:], start=True, stop=True),
                                  r=[k_.b, q_.b], w=[ps.b])
                            fw.op("act", lambda E: E.activation(out=e_.t[:, q0:], in_=ps.t[:, q0:], func=AF.Exp, scale=scale),
                                  r=[ps.b], w=[e_.b])
                            fw.op("act", lambda E: E.activation(out=sp_.t[:, q0:], in_=e_.t[:, q0:], func=AF.Ln, bias=1.0),
                                  r=[e_.b], w=[sp_.b])
                            if diag:
                                fw.op("pool", lambda E: E.tensor_tensor(out=sp_.t[:, q0:q0 + 128], in0=sp_.t[:, q0:q0 + 128],
                                                                        in1=C["mS"][0].t[:], op=ALU.mult),
                                      r=[sp_.b, C["mS"][0].b], w=[sp_.b])
                            fw.op("dve", lambda E: E.tensor_copy(out=hi_.t[:, q0:], in_=sp_.t[:, q0:]), r=[sp_.b], w=[hi_.b])
                            fw.op("dve", lambda E: E.tensor_tensor(out=lo_.t[:, q0:], in0=sp_.t[:, q0:], in1=hi_.t[:, q0:], op=ALU.subtract),
                                  r=[sp_.b, hi_.b], w=[lo_.b])

                        def stageB(c, slot):
                            q0, diag = geom(c, qt)
                            last = (c == 0)
                            if kind == "mla":
                                P = R["P"][slot % 3]
                                fw.op("pe", lambda E: E.matmul(O.t[:, q0:], lhsT=v_.t[:, c, :], rhs=P.t[:, q0:], start=False, stop=last),
                                      r=[v_.b, P.b], w=[O.b])
                                return
                            i2 = slot % 2
                            i3 = slot % NB
                            hi_, lo_ = (R[n_][i3] for n_ in ("hi", "lo"))
                            P = R["P"][i2]
                            for x_ in (hi_, lo_):
                                fw.op("pe", lambda E: E.matmul(A.t[:, q0:], lhsT=C["triIN"][1].t[:], rhs=x_.t[:, q0:], start=False, stop=False),
                                      r=[C["triIN"][1].b, x_.b], w=[A.b])
                            fw.op("pe", lambda E: E.matmul(A.t[:, q0:], lhsT=k_.t[:, c * 128:(c + 1) * 128], rhs=q_.t[:, q0:], start=False, stop=False),
                                  r=[k_.b, q_.b], w=[A.b])
                            fw.op("act", lambda E: E.activation(out=P.t[:, q0:], in_=A.t[:, q0:], func=AF.Exp), r=[A.b], w=[P.b])
                            if diag:
                                fw.op("pool", lambda E: E.tensor_tensor(out=P.t[:, q0:q0 + 128], in0=P.t[:, q0:q0 + 128],
                                                                        in1=C["mS"][1].t[:], op=ALU.mult),
                                      r=[P.b, C["mS"][1].b], w=[P.b])
                            yield_here.append(1)

                        def stageB2(c, slot):
                            q0, diag = geom(c, qt)
                            last = (c == 0)
                            i3 = slot % NB
                            hi_, lo_ = (R[n_][i3] for n_ in ("hi", "lo"))
                            if not last:
                                for x_ in (hi_, lo_):
                                    fw.op("pe", lambda E: E.matmul(A.t[:, q0:], lhsT=C["triSN"][1].t[:], rhs=x_.t[:, q0:], start=False, stop=False),
                                          r=[C["triSN"][1].b, x_.b], w=[A.b])
                                fw.op("pe", lambda E: E.matmul(A.t[:, q0:], lhsT=kn_.t[:, c * 128:(c + 1) * 128], rhs=q_.t[:, q0:], start=False, stop=False),
                                      r=[kn_.b, q_.b], w=[A.b])

                        def pv(c, slot):
                            q0, diag = geom(c, qt)
                            P = R["P"][slot % 2]
                            fw.op("pe", lambda E: E.matmul(O.t[:, q0:], lhsT=v_.t[:, c, :], rhs=P.t[:, q0:], start=False, stop=(c == 0)),
                                  r=[v_.b, P.b], w=[O.b])

                        yield_here = []
                        for l_ in range(min(LEAD, len(order))):
                            stageA(order[l_], cnt + l_)
                        for idx, c in enumerate(order):
                            slot = cnt + idx
                            if idx + LEAD < len(order):
                                stageA(order[idx + LEAD], slot + LEAD)
                            yield
                            if kind == "mla":
                                stageB(c, slot)
                            else:
                                stageB(c, slot)
                                yield
                                stageB2(c, slot)
                                if idx > 0:
                                    pv(order[idx - 1], slot - 1)
                        if kind == "sb":
                            pv(order[-1], cnt + len(order) - 1)
                        cnt += len(order)
                        o_ = R["o"][qt % 2]
                        if kind == "mla":
                            rs = R["rs"]
                            fw.op("pe", lambda E: E.matmul(A.t[:], lhsT=onesf.t[:], rhs=acc.t[:], start=True, stop=True),
                                  r=[onesf.b, acc.b], w=[A.b])
                            fw.op("dve", lambda E: E.reciprocal(out=rs.t[:], in_=A.t[:]), r=[A.b], w=[rs.b])
                            fw.op("dve", lambda E: E.tensor_tensor(out=o_.t[:], in0=O.t[:], in1=rs.t[:], op=ALU.mult),
                                  r=[O.b, rs.b], w=[o_.b])
                        else:
                            fw.op("act", lambda E: E.copy(out=o_.t[:], in_=O.t[:]), r=[O.b], w=[o_.b])
                        fw.dma("act", ymT[row0 + h * 128:row0 + (h + 1) * 128, qt * TT:(qt + 1) * TT], o_.t[:],
                               r=[o_.b], w=[self.db(ym_name, (row0 // 128 + h, qt))], key=o_.b)
                        yield

            gens = [stream(i) for i in range(NS)]
            alive = [True] * NS
            while any(alive):
                for i in range(NS):
                    if alive[i]:
                        try:
                            next(gens[i])
                        except StopIteration:
                            alive[i] = False
            self.barrier()

    def l0_inproj(self, xin, xin_name, P, O):
        fw = self.fw
        S = self.S
        with contextlib.ExitStack() as st:
            gb = self.bcast_row(st, "l0_g", P["mix_norm"], D)
            gq = self.bcast_row(st, "l0_gq", P["q_norm"], 512)
            gkv = self.bcast_row(st, "l0_gkv", P["kv_norm"], 512)
            dtb = self.bcast_row(st, "l0_dtb", P["dt_bias"], 32)
            ab = self.bcast_row(st, "l0_ab", P["a_log"], 32)
            fw.op("act", lambda E: E.activation(out=ab.t[:], in_=ab.t[:], func=AF.Exp), r=[ab.b], w=[ab.b])
            fw.op("dve", lambda E: E.tensor_scalar(out=ab.t[:], in0=ab.t[:], scalar1=-1.0, scalar2=None, op0=ALU.mult),
                  r=[ab.b], w=[ab.b])
            nt = self.norm_tiles(st, "l0")
            hTs = [st.enter_context(self.nc.sbuf_tensor(self.nm("l0_hT%d" % i), [128, 16, TT], BF16)) for i in range(2)]
            hTbs = [[[self.key("l0_hTb%d_%d_%d" % (i, s, h)) for h in range(2)] for s in range(NSUB)] for i in range(2)]
            pt = [self.tile(st, "l0_pt%d" % i, [128, 8, 128], BF16, psum=True) for i in range(2)]
            pp = [self.tile(st, "l0_pp%d" % i, [128, 512], F32, psum=True) for i in range(6)]
            pA = pp[0]
            evf = [self.tile(st, "l0_evf%d" % i, [128, 512], F32) for i in range(5)]
            evb = [self.tile(st, "l0_evb%d" % i, [128, 512], BF16) for i in range(6)]
            wsm = self.tile(st, "l0_wsm", [128, 16, 96], BF16)
            ws_, tiles_, _, _, _ = self.wts["wsm"]
            fw.dma("sp", wsm.t[:], ws_[0, 0], r=[tiles_[(0, 0)]], w=[wsm.b])
            cw_raw = self.tile(st, "l0_cwraw", [120, 128], F32)
            cwT = self.tile(st, "l0_cwT", [128, 120], F32)
            fw.dma("sp", cw_raw.t[0:96, :], P["conv_w"].rearrange("i (c p) -> (i c) p", p=128), r=[self.db("in_conv_w")], w=[cw_raw.b])
            fw.dma("sp", cw_raw.t[96:120, :], P["conv_b"].rearrange("(c p) -> c p", p=128), r=[self.db("in_conv_b")], w=[cw_raw.b])
            fw.op("pe", lambda E: E.transpose(out=pA.t[:, 0:120], in_=cw_raw.t[:], identity=self.identf.t[0:120, 0:120]),
                  r=[cw_raw.b, self.identf.b], w=[pA.b])
            fw.op("dve", lambda E: E.tensor_copy(out=cwT.t[:], in_=pA.t[:, 0:120]), r=[pA.b], w=[cwT.b])
            perm = self.tile(st, "l0_perm", [128, 128], F32)
            fw.op("pool", lambda E: E.memset(perm.t[:], 0.0), w=[perm.b])
            for (c0_, base_, fill_) in ((0, -32, -1.0), (32, 0, 1.0), (64, -96, -1.0), (96, -64, 1.0)):
                fw.op("pool", lambda E: E.affine_select(out=perm.t[:, c0_:c0_ + 32], in_=perm.t[:, c0_:c0_ + 32], pattern=[[-1, 32]],
                                                        compare_op=ALU.not_equal, fill=fill_, base=base_, channel_multiplier=1),
                      r=[perm.b], w=[perm.b])
            work = [self.tile(st, "l0_work%d" % i, [128, 515], F32) for i in range(2)]
            acc = [self.tile(st, "l0_acc%d" % i, [128, 512], F32) for i in range(2)]
            carry = self.tile(st, "l0_carry", [128, 24, 3], F32)
            fw.op("pool", lambda E: E.memset(carry.t[:], 0.0), w=[carry.b])
            dtt = [self.tile(st, "l0_dt%d_%d" % (i // 3, i % 3), [128, 32], F32) for i in range(6)]
            cn = [self.tile(st, "l0_cn%d" % i, [128, 512], BF16) for i in range(2)]
            cst = [[self.tile(st, "l0_cst%d_%d" % (i, j), [128, 1], F32) for j in range(3)] for i in range(2)]
            cjunk = self.tile(st, "l0_cjunk", [128, 512], BF16)
            cqT = st.enter_context(self.nc.sbuf_tensor(self.nm("l0_cqT"), [128, 4, TT], BF16))
            ckvT = st.enter_context(self.nc.sbuf_tensor(self.nm("l0_ckvT"), [128, 4, TT], BF16))
            cqTb = [[self.key("l0_cqTb%d" % s)] for s in range(NSUB)]
            ckvTb = [[self.key("l0_ckvTb%d" % s)] for s in range(NSUB)]
            AfK = [self.tile(st, "l0_AfK%d" % i, [128, 512], F32) for i in range(2)]
            AfQ = [self.tile(st, "l0_AfQ%d" % i, [128, 512], F32) for i in range(2)]
            for a__ in AfK + AfQ:
                fw.op("pool", lambda E: E.memset(a__.t[:], 0.0), w=[a__.b])
            r1 = [self.tile(st, "l0_r1_%d" % i, [128, 512], F32) for i in range(2)]
            r2 = [self.tile(st, "l0_r2_%d" % i, [128, 512], F32) for i in range(2)]
            rb = [self.tile(st, "l0_rb%d" % i, [128, 512], BF16) for i in range(2)]
            CC = [self.tile(st, "l0_CC%d" % i, [128, 512], F32) for i in range(2)]
            SS = [self.tile(st, "l0_SS%d" % i, [128, 512], F32) for i in range(2)]
            cnt = {"pp": 0, "evb": 0, "evf": 0, "rope": 0}

            def nxt(nm, lst):
                i = cnt[nm]
                cnt[nm] += 1
                return lst[i % len(lst)]

            def store_bf(p_ap, p_b, dst, dname, M=128, view=None):
                e_ = nxt("evb", evb)
                i = cnt["evb"]
                if i % 2 == 0:
                    fw.op("act", lambda E: E.copy(out=e_.t[0:M, :], in_=p_ap), r=[p_b], w=[e_.b])
                else:
                    fw.op("dve", lambda E: E.tensor_copy(out=e_.t[0:M, :], in_=p_ap), r=[p_b], w=[e_.b])
                src = e_.t[0:M, :] if view is None else view(e_.t[0:M, :])
                fw.dma("act", dst, src, r=[e_.b], w=[self.db(*dname)], key=e_.b)

            def rope(pa, lo, cc, ss, dst, dname):
                i = cnt["rope"] % 2
                cnt["rope"] += 1
                pB = nxt("pp", pp)
                Af = (AfK if lo == 0 else AfQ)[i]
                sl = slice(lo, lo + 64)
                fw.op("act", lambda E: E.copy(out=Af.t[sl, :], in_=pa.t[sl, :]), r=[pa.b], w=[Af.b])
                fw.op("pe", lambda E: E.matmul(pB.t[:], lhsT=perm.t[:], rhs=Af.t[:], start=True, stop=True),
                      r=[perm.b, Af.b], w=[pB.b])
                fw.op("dve", lambda E: E.tensor_tensor(out=r1[i].t[sl, :], in0=Af.t[sl, :], in1=cc.t[sl, :], op=ALU.mult),
                      r=[Af.b, cc.b], w=[r1[i].b])
                fw.op("dve", lambda E: E.tensor_tensor(out=r2[i].t[sl, :], in0=pB.t[sl, :], in1=ss.t[sl, :], op=ALU.mult),
                      r=[pB.b, ss.b], w=[r2[i].b])
                fw.op("pool", lambda E: E.tensor_tensor(out=rb[i].t[sl, :], in0=r1[i].t[sl, :], in1=r2[i].t[sl, :], op=ALU.add),
                      r=[r1[i].b, r2[i].b], w=[rb[i].b])
                fw.dma("act", dst, rb[i].t[sl, :], r=[rb[i].b], w=[self.db(*dname)], key=rb[i].b)

            self.norm_T(nt, xin, xin_name, 0, gb, hTs[0], hTbs[0], pt)
            for tt in range(S // TT):
                tsl = slice(tt * TT, (tt + 1) * TT)
                self.pump(3)
                hT = hTs[tt % 2]
                hTb = hTbs[tt % 2]
                hT_all = [b for l in hTb for b in l]
                cc = CC[tt % 2]
                ss = SS[tt % 2]
                for lo_ in (0, 64):
                    fw.dma("sp", cc.t[lo_:lo_ + 64, :], P["rope_cc"][:, tsl], r=[self.db("in_rope_cc")], w=[cc.b])
                    fw.dma("sp", ss.t[lo_:lo_ + 64, :], P["rope_ss"][:, tsl], r=[self.db("in_rope_ss")], w=[ss.b])
                for ct in range(4):
                    wt = self.wload("wz", ct)
                    for s in range(NSUB):
                        p_ = nxt("pp", pp)
                        for kc in range(16):
                            fw.op("pe", lambda E: E.matmul(p_.t[:], lhsT=hT[:, kc, s * 128:(s + 1) * 128], rhs=wt.t[:, kc, :],
                                                           start=(kc == 0), stop=(kc == 15)), r=[wt.b] + hT_all, w=[p_.b])
                        e_ = nxt("evf", evf)
                        fw.op("act", lambda E: E.activation(out=e_.t[:], in_=p_.t[:], func=AF.Silu), r=[p_.b], w=[e_.b])
                        row = tt * NSUB + s
                        fw.dma("act", O["zs"][row * 128:(row + 1) * 128, ct * 512:(ct + 1) * 512], e_.t[:],
                               r=[e_.b], w=[self.db("zs", (row, ct))], key=e_.b)
                for ct in range(6):
                    wt = self.wload("wxbc", ct)
                    for j in range(4):
                        ch = ct * 4 + j
                        p_ = nxt("pp", pp)
                        for kc in range(16):
                            fw.op("pe", lambda E: E.matmul(p_.t[:], lhsT=wt.t[:, kc, j * 128:(j + 1) * 128], rhs=hT[:, kc, :],
                                                           start=(kc == 0), stop=(kc == 15)), r=[wt.b] + hT_all, w=[p_.b])
                        w_ = work[ch % 2]
                        a_ = acc[ch % 2]
                        fw.op("act", lambda E: E.copy(out=w_.t[:, 3:515], in_=p_.t[:]), r=[p_.b], w=[w_.b])
                        fw.op("pool", lambda E: E.tensor_copy(out=w_.t[:, 0:3], in_=carry.t[:, ch, :]), r=[carry.b], w=[w_.b])
                        fw.op("dve", lambda E: E.tensor_scalar(out=a_.t[:], in0=w_.t[:, 0:512], scalar1=cwT.t[:, ch:ch + 1],
                                                               scalar2=cwT.t[:, 96 + ch:97 + ch], op0=ALU.mult, op1=ALU.add),
                              r=[w_.b, cwT.b], w=[a_.b])
                        for i in range(1, 4):
                            fw.op("dve", lambda E: E.scalar_tensor_tensor(out=a_.t[:], in0=w_.t[:, i:i + 512],
                                                                          scalar=cwT.t[:, i * 24 + ch:i * 24 + ch + 1], in1=a_.t[:],
                                                                          op0=ALU.mult, op1=ALU.add),
                                  r=[w_.b, cwT.b, a_.b], w=[a_.b])
                        fw.op("pool", lambda E: E.tensor_copy(out=carry.t[:, ch, :], in_=w_.t[:, 512:515]), r=[w_.b], w=[carry.b])
                        e_ = nxt("evb", evb)
                        fw.op("act", lambda E: E.activation(out=e_.t[:], in_=a_.t[:], func=AF.Silu), r=[a_.b], w=[e_.b])
                        fw.dma("act", O["xbcT"][ch * 128:(ch + 1) * 128, tsl], e_.t[:], r=[e_.b], w=[self.db("xbcT", (ch, tt))], key=e_.b)
                if tt + 1 < S // TT:
                    self.norm_T(nt, xin, xin_name, tt + 1, gb, hTs[(tt + 1) % 2], hTbs[(tt + 1) % 2], pt)
                for s in range(NSUB):
                    row = tt * NSUB + s
                    p_ = nxt("pp", pp)
                    for kc in range(16):
                        fw.op("pe", lambda E: E.matmul(p_.t[:, 0:32], lhsT=hT[:, kc, s * 128:(s + 1) * 128], rhs=wsm.t[:, kc, 64:96],
                                                       start=(kc == 0), stop=(kc == 15)), r=[wsm.b] + hT_all, w=[p_.b])
                    d0, d1, d2 = dtt[(row % 2) * 3:(row % 2) * 3 + 3]
                    fw.op("dve", lambda E: E.tensor_tensor(out=d0.t[:], in0=p_.t[:, 0:32], in1=dtb.t[:], op=ALU.add), r=[p_.b, dtb.b], w=[d0.b])
                    fw.op("act", lambda E: E.activation(out=d0.t[:], in_=d0.t[:], func=AF.Exp), r=[d0.b], w=[d0.b])
                    fw.op("act", lambda E: E.activation(out=d1.t[:], in_=d0.t[:], func=AF.Ln, bias=1.0), r=[d0.b], w=[d1.b])
                    fw.op("dve", lambda E: E.tensor_tensor(out=d2.t[:], in0=d1.t[:], in1=ab.t[:], op=ALU.mult), r=[d1.b, ab.b], w=[d2.b])
                    fw.dma("act", O["dt"][row * 128:(row + 1) * 128, :], d1.t[:], r=[d1.b], w=[self.db("dt", row)], key=d1.b)
                    fw.dma("act", O["adt"][row * 128:(row + 1) * 128, :], d2.t[:], r=[d2.b], w=[self.db("adt", row)], key=d2.b)
                for (wn, gnb, cT, cTb) in (("wcq", gq, cqT, cqTb), ("wckv", gkv, ckvT, ckvTb)):
                    wt = self.wload(wn, 0)
                    for s in range(NSUB):
                        p_ = nxt("pp", pp)
                        for kc in range(16):
                            fw.op("pe", lambda E: E.matmul(p_.t[:], lhsT=hT[:, kc, s * 128:(s + 1) * 128], rhs=wt.t[:, kc, :],
                                                           start=(kc == 0), stop=(kc == 15)), r=[wt.b] + hT_all, w=[p_.b])
                        c_ = cn[s % 2]
                        cf_ = nxt("evf", evf)
                        fw.op("act", lambda E: E.copy(out=cf_.t[:], in_=p_.t[:]), r=[p_.b], w=[cf_.b])
                        self.rmsnorm(cf_.t[:], cf_.b, 512, gnb, c_.t[:], c_.b, cjunk, cst[s % 2])
                        self.transpose_to(c_.t, c_.b, 4, cT, cTb[s], 0, s * 128, pt, flip=s)
                cq_all = [l[0] for l in cqTb]
                ckv_all = [l[0] for l in ckvTb]
                pAk = nxt("pp", pp)
                for kc in range(16):
                    fw.op("pe", lambda E: E.matmul(pAk.t[0:96, :], lhsT=wsm.t[:, kc, 0:96], rhs=hT[:, kc, :],
                                                   start=(kc == 0), stop=(kc == 15)), r=[wsm.b] + hT_all, w=[pAk.b])
                pend = [(pAk, 0, O["kpT"][:, tsl], ("kpT", tt))]
                for t2 in range(2):
                    wt = self.wload("wuq", t2)
                    for hh in range(8):
                        h = t2 * 8 + hh
                        c0 = hh * 192
                        p_ = nxt("pp", pp)
                        for kc in range(4):
                            fw.op("pe", lambda E: E.matmul(p_.t[:], lhsT=wt.t[:, kc, c0:c0 + 128], rhs=cqT[:, kc, :],
                                                           start=(kc == 0), stop=(kc == 3)), r=[wt.b] + cq_all, w=[p_.b])
                        store_bf(p_.t[:], p_.b, O["qnT"][h, :, tsl], ("qnT", (h, tt)))
                        pA_ = nxt("pp", pp)
                        for kc in range(4):
                            fw.op("pe", lambda E: E.matmul(pA_.t[:], lhsT=wt.t[:, kc, c0 + 64:c0 + 192], rhs=cqT[:, kc, :],
                                                           start=(kc == 0), stop=(kc == 3)), r=[wt.b] + cq_all, w=[pA_.b])
                        pa0, lo0, dst0, dn0 = pend.pop(0)
                        rope(pa0, lo0, cc, ss, dst0, dn0)
                        pend.append((pA_, 64, O["qpT"][h, :, tsl], ("qpT", (h, tt))))
                pa0, lo0, dst0, dn0 = pend.pop(0)
                rope(pa0, lo0, cc, ss, dst0, dn0)
                for t2 in range(2):
                    wt = self.wload("wukv", t2)
                    for hh in range(8):
                        h = t2 * 8 + hh
                        c0 = hh * 256
                        p_ = nxt("pp", pp)
                        for kc in range(4):
                            fw.op("pe", lambda E: E.matmul(p_.t[:], lhsT=wt.t[:, kc, c0:c0 + 128], rhs=ckvT[:, kc, :],
                                                           start=(kc == 0), stop=(kc == 3)), r=[wt.b] + ckv_all, w=[p_.b])
                        store_bf(p_.t[:], p_.b, O["knT"][h, :, tsl], ("knT", (h, tt)))
                    for hg in range(2):
                        for s in range(NSUB):
                            row = tt * NSUB + s
                            p_ = nxt("pp", pp)
                            for kc in range(4):
                                rhs = wt.t[:, kc, :].rearrange("p (h two d) -> p h two d", two=2, d=128)[:, hg * 4:(hg + 1) * 4, 1, :]
                                fw.op("pe", lambda E: E.matmul(p_.t[:], lhsT=ckvT[:, kc, s * 128:(s + 1) * 128], rhs=rhs,
                                                               start=(kc == 0), stop=(kc == 3)), r=[wt.b] + ckv_all, w=[p_.b])
                            h0 = t2 * 8 + hg * 4
                            dst = O["vv"][h0:h0 + 4, row * 128:(row + 1) * 128, :].rearrange("h t d -> t h d")
                            store_bf(p_.t[:], p_.b, dst, ("vv0", (h0, row)), view=lambda a: a.rearrange("t (h d) -> t h d", h=4))
            self.barrier()

    def l0_ssd(self, P, O, ymT, ym_name):
        fw = self.fw
        S = self.S
        NCK = S // 128
        xbcT = O["xbcT"]
        with contextlib.ExitStack() as st:
            gn = self.bcast_row(st, "sd_gn", P["ssd_norm"], D)
            dsk = self.bcast_row(st, "sd_dsk", P["d_skip"], 32)
            def mkf(name, pattern, cm, cmp_):
                f = self.tile(st, "sd_" + name, [128, 128], F32)
                fw.op("pool", lambda E: E.memset(f.t[:], 1.0), w=[f.b])
                if pattern is not None:
                    fw.op("pool", lambda E: E.affine_select(out=f.t[:], in_=f.t[:], pattern=pattern, compare_op=cmp_, fill=0.0,
                                                            base=0, channel_multiplier=cm), r=[f.b], w=[f.b])
                return f
            U1 = mkf("U1", [[-1, 128]], 1, ALU.is_gt)
            U2 = mkf("U2", [[1, 128]], -1, ALU.is_ge)
            onesf = mkf("ones", None, 0, None)
            xT = [self.tile(st, "sd_xT%d" % i, [128, 16, 128], BF16) for i in range(2)]
            bcT = [self.tile(st, "sd_bcT%d" % i, [128, 8, 128], BF16) for i in range(2)]
            dtc = [self.tile(st, "sd_dt%d" % i, [128, 32], F32) for i in range(2)]
            adc = [self.tile(st, "sd_ad%d" % i, [128, 32], F32) for i in range(2)]
            zsc = [self.tile(st, "sd_zs%d" % i, [128, D], F32) for i in range(1)]
            xtok = [self.tile(st, "sd_xtok%d" % i, [128, D], BF16) for i in range(2)]
            btok = [self.tile(st, "sd_btok%d" % i, [128, 4, 128], BF16) for i in range(2)]
            xdt = [self.tile(st, "sd_xdt%d" % i, [128, D], BF16) for i in range(2)]
            xdte = [self.tile(st, "sd_xdte%d" % i, [128, D], BF16) for i in range(1)]
            Rm = self.tile(st, "sd_R", [128, 32, 128], F32)
            eseg = [self.tile(st, "sd_eseg%d" % i, [128, 512], F32) for i in range(2)]
            MT = [self.tile(st, "sd_MT%d" % i, [128, 32, 128], BF16) for i in range(1)]
            cbm = [self.tile(st, "sd_cbm%d" % i, [128, 4, 128], F32) for i in range(2)]
            sm = [[self.tile(st, "sd_sm%d_%d" % (i, j), [128, 32], F32) for j in range(4)] for i in range(2)]
            yA = [self.tile(st, "sd_yA%d" % i, [128, D], F32) for i in range(1)]
            yB = self.tile(st, "sd_yB", [128, D], F32)
            yo = [self.tile(st, "sd_yo%d" % i, [128, D], BF16) for i in range(1)]
            junk = self.tile(st, "sd_junk", [128, D], BF16)
            stt = [[self.tile(st, "sd_st%d_%d" % (i, j), [128, 1], F32) for j in range(3)] for i in range(2)]
            yT = st.enter_context(self.nc.sbuf_tensor(self.nm("sd_yT"), [128, 16, TT], BF16))
            yTb = [[self.key("sd_yTb%d_%d" % (s, h)) for h in range(2)] for s in range(NSUB)]
            prev = self.tile(st, "sd_prev", [128, 4, 512], F32)
            prevb = self.tile(st, "sd_prevb", [128, 4, 512], BF16)
            ptmp = self.tile(st, "sd_ptmp", [128, 512], F32)
            fw.op("pool", lambda E: E.memset(prev.t[:], 0.0), w=[prev.b])
            fw.op("pool", lambda E: E.memset(prevb.t[:], 0.0), w=[prevb.b])
            pt = self.tile(st, "sd_pt", [128, 8, 128], BF16, psum=True)
            Y = [self.tile(st, "sd_Y%d" % i, [128, 512], F32, psum=True) for i in range(4)]
            pCB = self.tile(st, "sd_pCB", [128, 4, 128], F32, psum=True)
            pSeg = self.tile(st, "sd_pSeg", [128, 512], F32, psum=True)
            pM = self.tile(st, "sd_pM", [128, 512], F32, psum=True)

            def bc_h(ap32, n):
                return ap32.unsqueeze(2).broadcast_to([128, ap32.shape[1], n])

            for c in range(NCK):
                i2 = c % 2
                self.pump(1)
                csl = slice(c * 128, (c + 1) * 128)
                x_, b_, d_, a_, z_ = xT[i2], bcT[i2], dtc[i2], adc[i2], zsc[0]
                fw.dma("sp", x_.t[:], xbcT[0:2048, csl].rearrange("(k p) t -> p k t", p=128), r=[self.db("xbcT_all")], w=[x_.b])
                fw.dma("sp", b_.t[:], xbcT[2048:3072, csl].rearrange("(k p) t -> p k t", p=128), r=[self.db("xbcT_all")], w=[b_.b])
                fw.dma("sp", d_.t[:], O["dt"][csl, :], r=[self.db("dt_all")], w=[d_.b])
                fw.dma("sp", a_.t[:], O["adt"][csl, :], r=[self.db("adt_all")], w=[a_.b])
                fw.dma("sp", z_.t[:], O["zs"][csl, :], r=[self.db("zs_all")], w=[z_.b])
                xt_, bt_, xd_, xde_ = xtok[i2], btok[i2], xdt[i2], xdte[0]
                for g0 in range(0, 16, 8):
                    for j in range(8):
                        fw.op("pe", lambda E: E.transpose(out=pt.t[:, j, :], in_=x_.t[:, g0 + j, :], identity=self.ident.t[:]),
                              r=[x_.b, self.ident.b], w=[pt.b])
                    if g0 == 0:
                        fw.op("act", lambda E: E.copy(out=xt_.t[:, 0:1024], in_=pt.t[:].rearrange("p a b -> p (a b)")), r=[pt.b], w=[xt_.b])
                    else:
                        fw.op("dve", lambda E: E.tensor_copy(out=xt_.t[:, 1024:2048], in_=pt.t[:].rearrange("p a b -> p (a b)")),
                              r=[pt.b], w=[xt_.b])
                for j in range(4):
                    fw.op("pe", lambda E: E.transpose(out=pt.t[:, j, :], in_=b_.t[:, j, :], identity=self.ident.t[:]),
                          r=[b_.b, self.ident.b], w=[pt.b])
                fw.op("act", lambda E: E.copy(out=bt_.t[:], in_=pt.t[:, 0:4, :]), r=[pt.b], w=[bt_.b])
                acs, tot, eacs, dte = sm[i2]
                fw.op("pe", lambda E: E.matmul(pSeg.t[:, 0:32], lhsT=U2.t[:], rhs=a_.t[:], start=True, stop=True),
                      r=[U2.b, a_.b], w=[pSeg.b])
                fw.op("pe", lambda E: E.matmul(pSeg.t[:, 32:64], lhsT=onesf.t[:], rhs=a_.t[:], start=True, stop=True),
                      r=[onesf.b, a_.b], w=[pSeg.b])
                fw.op("act", lambda E: E.activation(out=eacs.t[:], in_=pSeg.t[:, 0:32], func=AF.Exp), r=[pSeg.b], w=[eacs.b])
                fw.op("dve", lambda E: E.tensor_copy(out=acs.t[:], in_=pSeg.t[:, 0:32]), r=[pSeg.b], w=[acs.b])
                fw.op("dve", lambda E: E.tensor_tensor(out=dte.t[:], in0=pSeg.t[:, 32:64], in1=acs.t[:], op=ALU.subtract),
                      r=[pSeg.b, acs.b], w=[dte.b])
                fw.op("act", lambda E: E.activation(out=dte.t[:], in_=dte.t[:], func=AF.Exp), r=[dte.b], w=[dte.b])
                fw.op("act", lambda E: E.activation(out=tot.t[:], in_=pSeg.t[:, 32:64], func=AF.Exp), r=[pSeg.b], w=[tot.b])
                fw.op("dve", lambda E: E.tensor_tensor(out=xd_.t[:].rearrange("p (h d) -> p h d", d=64),
                                                       in0=xt_.t[:].rearrange("p (h d) -> p h d", d=64), in1=bc_h(d_.t[:], 64), op=ALU.mult),
                      r=[xt_.b, d_.b], w=[xd_.b])
                fw.op("pool", lambda E: E.tensor_tensor(out=xde_.t[:].rearrange("p (h d) -> p h d", d=64),
                                                        in0=xd_.t[:].rearrange("p (h d) -> p h d", d=64), in1=bc_h(dte.t[:], 64), op=ALU.mult),
                      r=[xd_.b, dte.b], w=[xde_.b])
                fw.op("dve", lambda E: E.tensor_tensor(out=Rm.t[:], in0=U2.t[:].unsqueeze(1).broadcast_to([128, 32, 128]),
                                                       in1=bc_h(a_.t[:], 128), op=ALU.mult), r=[U2.b, a_.b], w=[Rm.b])
                for g in range(4):
                    fw.op("pe", lambda E: E.matmul(pCB.t[:, g, :], lhsT=b_.t[:, g, :], rhs=b_.t[:, 4 + g, :], start=True, stop=True),
                          r=[b_.b], w=[pCB.b])
                cb_ = cbm[i2]
                fw.op("dve", lambda E: E.tensor_tensor(out=cb_.t[:], in0=pCB.t[:], in1=U2.t[:].unsqueeze(1).broadcast_to([128, 4, 128]),
                                                       op=ALU.mult), r=[pCB.b, U2.b], w=[cb_.b])
                mt_ = MT[0]
                for q4 in range(8):
                    g = q4 // 2
                    es_ = eseg[q4 % 2]
                    psg = pSeg if q4 % 2 == 0 else Y[0]
                    fw.op("pe", lambda E: E.matmul(psg.t[:], lhsT=U1.t[:], rhs=Rm.t[:, q4 * 4:(q4 + 1) * 4, :], start=True, stop=True),
                          r=[U1.b, Rm.b], w=[psg.b])
                    fw.op("act", lambda E: E.activation(out=es_.t[:], in_=psg.t[:], func=AF.Exp), r=[psg.b], w=[es_.b])
                    fw.op("dve", lambda E: E.tensor_tensor(out=mt_.t[:, q4 * 4:(q4 + 1) * 4, :],
                                                           in0=es_.t[:].rearrange("p (h l) -> p h l", l=128),
                                                           in1=cb_.t[:, g:g + 1, :].broadcast_to([128, 4, 128]), op=ALU.mult),
                          r=[es_.b, cb_.b], w=[mt_.b])
                for h in range(32):
                    Yb = Y[h // 8]
                    fw.op("pe", lambda E: E.matmul(Yb.t[:, (h % 8) * 64:(h % 8 + 1) * 64], lhsT=mt_.t[:, h, :], rhs=xd_.t[:, h * 64:(h + 1) * 64],
                                                   start=True, stop=True), r=[mt_.b, xd_.b], w=[Yb.b])
                ya_ = yA[0]
                fw.op("pool", lambda E: E.tensor_tensor(out=yB.t[:].rearrange("p (h d) -> p h d", d=64),
                                                        in0=xt_.t[:].rearrange("p (h d) -> p h d", d=64), in1=bc_h(dsk.t[:], 64), op=ALU.mult),
                      r=[xt_.b, dsk.b], w=[yB.b])
                for g in range(4):
                    gs = slice(g * 512, (g + 1) * 512)
                    fw.op("pe", lambda E: E.matmul(pM.t[:], lhsT=b_.t[:, 4 + g, :], rhs=prevb.t[:, g, :], start=True, stop=True),
                          r=[b_.b, prevb.b], w=[pM.b])
                    fw.op("dve", lambda E: E.tensor_tensor(out=ya_.t[:, gs].rearrange("p (h d) -> p h d", d=64),
                                                           in0=pM.t[:].rearrange("p (h d) -> p h d", d=64),
                                                           in1=bc_h(eacs.t[:, g * 8:(g + 1) * 8], 64), op=ALU.mult),
                          r=[pM.b, eacs.b], w=[ya_.b])
                    fw.op("dve", lambda E: E.tensor_tensor(out=ya_.t[:, gs], in0=Y[g].t[:], in1=ya_.t[:, gs], op=ALU.add),
                          r=[Y[g].b, ya_.b], w=[ya_.b])
                    fw.op("pe", lambda E: E.matmul(pM.t[:], lhsT=bt_.t[:, g, :], rhs=xde_.t[:, gs], start=True, stop=True),
                          r=[bt_.b, xde_.b], w=[pM.b])
                    fw.op("dve", lambda E: E.tensor_tensor(out=ptmp.t[:].rearrange("p (h d) -> p h d", d=64),
                                                           in0=prev.t[:, g, :].rearrange("p (h d) -> p h d", d=64),
                                                           in1=bc_h(tot.t[:, g * 8:(g + 1) * 8], 64), op=ALU.mult),
                          r=[prev.b, tot.b], w=[ptmp.b])
                    fw.op("dve", lambda E: E.tensor_tensor(out=prev.t[:, g, :], in0=pM.t[:], in1=ptmp.t[:], op=ALU.add),
                          r=[pM.b, ptmp.b], w=[prev.b])
                    fw.op("act", lambda E: E.copy(out=prevb.t[:, g, :], in_=prev.t[:, g, :]), r=[prev.b], w=[prevb.b])
                fw.op("pool", lambda E: E.tensor_tensor(out=ya_.t[:], in0=ya_.t[:], in1=yB.t[:], op=ALU.add), r=[ya_.b, yB.b], w=[ya_.b])
                fw.op("dve", lambda E: E.tensor_tensor(out=ya_.t[:], in0=ya_.t[:], in1=z_.t[:], op=ALU.mult), r=[ya_.b, z_.b], w=[ya_.b])
                yo_ = yo[0]
                self.rmsnorm(ya_.t[:], ya_.b, D, gn, yo_.t[:], yo_.b, junk, stt[i2])
                s4 = c % NSUB
                self.transpose_to(yo_.t, yo_.b, 16, yT, yTb[s4], 0, s4 * 128, pt, flip=c)
                if s4 == NSUB - 1:
                    tt = c // NSUB
                    allb = [b for l in yTb for b in l]
                    fw.dma("act", ymT[0:2048, tt * TT:(tt + 1) * TT].rearrange("(k p) t -> p k t", p=128), yT[:],
                           r=allb, w=[self.db(ym_name, ("ssd", tt))], key=allb[0])
            self.barrier()


IN_OFF = (0, 2048, 5120, 5152, 5664, 6176, 6240)


def model_decl(S):
    return {
        "x": (S, D), "mix_norm": (2, D), "ffn_norm": (2, D), "w_in": (1, D, 6240), "conv_w": (1, 4, 3072),
        "conv_b": (1, 3072), "dt_bias": (1, 32), "a_log": (1, 32), "d_skip": (1, 32), "ssd_norm": (1, D),
        "q_norm": (1, 512), "kv_norm": (1, 512), "w_uq": (1, 512, 3072), "w_ukv": (1, 512, 4096),
        "w_out_even": (1, 4096, D), "w_qkv": (1, D, 3 * D), "w_out_odd": (1, D, D),
        "w_gate": (2, D, FH), "w_up": (2, D, FH), "w_down": (2, FH, D), "final_norm": (1, D),
        "rope_cc": (64, S), "rope_ss": (64, S),
    }


def prep_l0_weights(bld):
    I = bld.inp
    w_in = I["w_in"][0]
    bld.prep_w("wz", w_in[:, 0:2048])
    bld.prep_w("wxbc", w_in[:, 2048:5120])
    bld.prep_w("wcq", w_in[:, 5152:5664])
    bld.prep_w("wckv", w_in[:, 5664:6176])
    ws = bld.dram("ws_wsm", [1, 1, 128, 16, 96], BF16)
    tb = Buf("w_wsm")
    k_ = bld.cast_dma(ws[0, 0, :, :, 64:96], w_in[:, 5120:5152].rearrange("(kc p) n -> p kc n", p=128), bld.db("in_w_in"), tb)
    bld.cast_dma(ws[0, 0, :, :, 0:64], w_in[:, 6176:6240].rearrange("(kc p) n -> p kc n", p=128), bld.db("in_w_in"), tb, same_key=k_)
    bld.wts["wsm"] = (ws, {(0, 0): tb}, 16, 1, 1)
    bld.prep_w("wuq", I["w_uq"][0], cw=1536)
    bld.prep_w("wukv", I["w_ukv"][0], cw=2048)
    bld.prep_w("wo0", I["w_out_even"][0])


def l0_scratch(bld):
    S = bld.S
    O = {}
    O["zs"] = bld.dram("sc_zs", [S, D], F32)
    O["xbcT"] = bld.dram("sc_xbcT", [3072, S], BF16)
    O["dt"] = bld.dram("sc_dt", [S, 32], F32)
    O["adt"] = bld.dram("sc_adt", [S, 32], F32)
    O["kpT"] = bld.dram("sc_kpT", [64, S], BF16)
    O["qnT"] = bld.dram("sc_qnT", [16, 128, S], BF16)
    O["qpT"] = bld.dram("sc_qpT", [16, 64, S], BF16)
    O["knT"] = bld.dram("sc_knT", [16, 128, S], BF16)
    O["vv"] = bld.dram("sc_vv0", [16, S, 128], BF16)
    return O


def l0_params(bld):
    I = bld.inp
    return {"mix_norm": I["mix_norm"][0, :], "q_norm": I["q_norm"][0, :], "kv_norm": I["kv_norm"][0, :], "dt_bias": I["dt_bias"][0, :],
            "a_log": I["a_log"][0, :], "conv_w": I["conv_w"][0], "conv_b": I["conv_b"][0, :], "rope_cc": I["rope_cc"], "rope_ss": I["rope_ss"],
            "ssd_norm": I["ssd_norm"][0, :], "d_skip": I["d_skip"][0, :]}


def layer0_mixer(bld, xin, xin_name, xout, xout_name, ym0):
    O = l0_scratch(bld)
    P = l0_params(bld)
    bld.l0_inproj(xin, xin_name, P, O)
    bld.l0_ssd(P, O, ym0, "ym0")
    bld.attention("mla", 16, O["qnT"], O["knT"], O["vv"], ym0, "ym0", 2048, 192 ** -0.5, qpT=O["qpT"], kpT=O["kpT"])
    bld.out_proj("wo0", 32, ym0, "ym0", xin, xin_name, xout, xout_name)


def rope_tables_np(S):
    half = 32
    inv_freq = (np.float32(10000.0) ** (-np.arange(half, dtype=np.float32) / np.float32(half))).astype(np.float32)
    ang = (np.arange(S, dtype=np.float32)[:, None] * inv_freq[None, :]).astype(np.float32)
    cos = np.cos(ang).astype(np.float32).T
    sin = np.sin(ang).astype(np.float32).T
    return np.ascontiguousarray(np.concatenate([cos, cos], 0)), np.ascontiguousarray(np.concatenate([sin, sin], 0))


def prep_rest_weights(bld):
    I = bld.inp
    bld.prep_w("wg0", I["w_gate"][0], defer=True)
    bld.prep_w("wu0", I["w_up"][0], defer=True)
    bld.prep_w("wd0", I["w_down"][0], kcg_max=11, defer=True)
    bld.prep_w("wqkv", I["w_qkv"][0], defer=True)
    bld.prep_w("wo1", I["w_out_odd"][0], defer=True)
    bld.prep_w("wg1", I["w_gate"][1], defer=True)
    bld.prep_w("wu1", I["w_up"][1], defer=True)
    bld.prep_w("wd1", I["w_down"][1], kcg_max=11, defer=True)


def build_model(S):
    bld = Builder(S, model_decl(S))
    I = bld.inp
    with bld.root as st:
        out = bld.dram("out", [S, D], F32, kind="ExternalOutput")
        x1 = bld.dram("sc_x1", [S, D], F32)
        x2 = bld.dram("sc_x2", [S, D], F32)
        x3 = bld.dram("sc_x3", [S, D], F32)
        x4 = bld.dram("sc_x4", [S, D], F32)
        ym0 = bld.dram("sc_ym0", [4096, S], BF16)
        ym1 = bld.dram("sc_ym1", [2048, S], BF16)
        qT = bld.dram("sc_qT", [16, 128, S], BF16)
        kT = bld.dram("sc_kT", [16, 128, S], BF16)
        vv = bld.dram("sc_vv1", [16, S, 128], BF16)
        bld.setup_consts(st)
        prep_l0_weights(bld)
        prep_rest_weights(bld)
        O = l0_scratch(bld)
        P = l0_params(bld)
        bld.l0_inproj(I["x"], "x", P, O)
        bld.l0_ssd(P, O, ym0, "ym0")
        bld.attention("mla", 16, O["qnT"], O["knT"], O["vv"], ym0, "ym0", 2048, 192 ** -0.5, qpT=O["qpT"], kpT=O["kpT"])
        bld.flush_casts()
        bld.out_proj("wo0", 32, ym0, "ym0", I["x"], "x", x1, "x1")
        bld.ffn(0, x1, "x1", x2, "x2", I["ffn_norm"][0, :])
        bld.sb_qkv(x2, "x2", I["mix_norm"][1, :], qT, kT, vv, 128 ** -0.5)
        bld.attention("sb", 16, qT, kT, vv, ym1, "ym1", 0, 1.0)
        bld.out_proj("wo1", 16, ym1, "ym1", x2, "x2", x3, "x3")
        bld.ffn(1, x3, "x3", x4, "x4", I["ffn_norm"][1, :])
        bld.final_norm(x4, "x4", out, "out", I["final_norm"][0, :])
    return bld


N_CORES = 4


def kernel(**inputs):
    x = np.asarray(inputs["x"], dtype=np.float32)
    B, S, _ = x.shape
    cc, ss = rope_tables_np(S)
    shared = {}
    for k, shp in model_decl(S).items():
        if k in ("x", "rope_cc", "rope_ss"):
            continue
        shared[k] = np.ascontiguousarray(np.asarray(inputs[k], dtype=np.float32).reshape(shp))
    shared["rope_cc"] = cc
    shared["rope_ss"] = ss
    bld = build_model(S)
    in_maps = []
    for b in range(B):
        m = dict(shared)
        m["x"] = np.ascontiguousarray(x[b])
        in_maps.append(m)
    res = run_bass_kernel_spmd(bld.nc, in_maps, core_ids=list(range(B)))
    return np.stack([np.asarray(res.results[b]["out"], dtype=np.float32) for b in range(B)], axis=0)
```

```python
import contextlib
import numpy as np
import concourse.bass as bass
import concourse.mybir as mybir
from concourse.bass_utils import run_bass_kernel_spmd

F32 = mybir.dt.float32
BF16 = mybir.dt.bfloat16
AF = mybir.ActivationFunctionType
ALU = mybir.AluOpType
AX = mybir.AxisListType


class Buf:
    __slots__ = ("name", "lw", "rd", "sem", "n", "base", "persist", "dead")

    def __init__(self, name=""):
        self.name = name
        self.lw = None
        self.rd = []
        self.sem = None
        self.n = 0
        self.base = 0
        self.persist = False
        self.dead = False


class Fw:
    CE = ("pe", "act", "dve", "pool")

    def __init__(self, nc, stack):
        self.nc = nc
        self.stack = stack
        self.eng = {"pe": nc.tensor, "act": nc.scalar, "dve": nc.vector, "pool": nc.gpsimd, "sp": nc.sync}
        self.cnt = {e: 0 for e in self.CE}
        self.sems = {e: stack.enter_context(nc.semaphore("s_" + e)) for e in self.CE}
        self.waited = {e: {} for e in self.eng}
        self.nsem = 4
        self.ninst = 0
        self.swq = []
        self.free_sems = []

    def sbuf(self, name, shape, dt):
        return self.stack.enter_context(self.nc.sbuf_tensor(name, shape, dt))

    def psum(self, name, shape, dt):
        return self.stack.enter_context(self.nc.psum_tensor(name, shape, dt))

    def _semobj(self, key):
        if isinstance(key, str):
            return self.sems[key]
        if key.sem is None:
            if self.free_sems:
                key.sem, key.base = self.free_sems.pop()
            else:
                key.sem = self.stack.enter_context(self.nc.semaphore("d%d" % self.nsem))
                key.base = 0
                self.nsem += 1
        return key.sem

    def release(self, key):
        if key.sem is not None:
            self.free_sems.append((key.sem, key.base + 16 * key.n))
            key.sem = None

    def _deps(self, eng, r, w):
        need = {}
        for b in r:
            if b.lw is not None:
                k, v = b.lw
                if need.get(k, 0) < v:
                    need[k] = v
        for b in w:
            if b.lw is not None:
                k, v = b.lw
                if need.get(k, 0) < v:
                    need[k] = v
            for (k, v) in b.rd:
                if need.get(k, 0) < v:
                    need[k] = v
        return need

    def _wait(self, eng, need, skip_key=None):
        E = self.eng[eng]
        wd = self.waited[eng]
        for k, v in need.items():
            if k is skip_key:
                continue
            if not isinstance(k, str) and k.dead:
                continue
            kk = k if isinstance(k, str) else id(k)
            if wd.get(kk, 0) >= v:
                continue
            E.wait_ge(self._semobj(k), v)
            wd[kk] = v
            self.ninst += 1

    def _commit(self, tok, r, w):
        for b in r:
            b.rd.append(tok)
            if len(b.rd) > 64:
                m = {}
                for (k, v) in b.rd:
                    kk = k if isinstance(k, str) else id(k)
                    if kk not in m or m[kk][1] < v:
                        m[kk] = (k, v)
                b.rd = list(m.values())
        for b in w:
            b.lw = tok
            b.rd = []

    def op(self, eng, fn, r=(), w=()):
        need = self._deps(eng, r, w)
        if eng == "pe":
            need.pop("pe", None)
        else:
            pass
        self._wait(eng, need)
        ins = fn(self.eng[eng])
        self.cnt[eng] += 1
        ins.then_inc(self.sems[eng], 1)
        self.ninst += 1
        tok = (eng, self.cnt[eng])
        self._commit(tok, r, w)
        return tok

    def dma(self, q, out, in_, r=(), w=(), key=None, **kw):
        if key is None:
            key = w[0]
        need = self._deps(q, r, w)
        self._wait(q, need, skip_key=key if self._same_key_ok(key, r, w) else None)
        sem = self._semobj(key)
        key.n += 1
        ins = self.eng[q].dma_start(out=out, in_=in_, **kw)
        ins.then_inc(sem, 16)
        self.ninst += 1
        tok = (key, key.base + 16 * key.n)
        self._commit(tok, r, w)
        return tok

    def dmalike(self, q, fn, r=(), w=(), key=None):
        need = self._deps(q, r, w)
        self._wait(q, need)
        sem = self._semobj(key)
        key.n += 1
        ins = fn(self.eng[q])
        ins.then_inc(sem, 16)
        self.ninst += 1
        tok = (key, key.base + 16 * key.n)
        self._commit(tok, r, w)
        return tok

    def _same_key_ok(self, key, r, w):
        for b in r:
            if b.lw is not None and b.lw[0] is key:
                return False
        for b in w:
            for (k, v) in b.rd:
                if k is key:
                    return False
        return True

    def barrier_wait(self, eng, toks):
        need = {}
        for (k, v) in toks:
            if need.get(k, 0) < v:
                need[k] = v
        self._wait(eng, need)


D = 2048
FH = 5632
EPS = 1e-6
TT = 512
NSUB = TT // 128


class T:
    __slots__ = ("t", "b")

    def __init__(self, t, b):
        self.t = t
        self.b = b


class Builder:
    def __init__(self, S, inputs_decl):
        self.S = S
        self.nc = nc = bass.Bass("TRN2", target_bir_lowering=False)
        self.root = contextlib.ExitStack()
        self.fw = Fw(nc, self.root)
        self.inp = {}
        for name, shape in inputs_decl.items():
            self.inp[name] = nc.dram_tensor(name, list(shape), F32, kind="ExternalInput").ap()
        self.dbufs = {}
        self.wts = {}
        self.scr = {}
        self.keys = []
        self.dead = []
        self.persist = True
        self.wslot_i = 0
        self.cast_q = []
        self.cast_keys = []
        self.cast_i = 0
        self.uid = 0
        self.marks = []

    def db(self, name, idx=0):
        k = (name, idx)
        b = self.dbufs.get(k)
        if b is None:
            b = self.dbufs[k] = Buf("%s_%s" % (name, idx))
        return b

    def key(self, name):
        b = Buf(name)
        b.persist = self.persist
        self.keys.append(b)
        return b

    def tile(self, st, name, shape, dt, psum=False):
        t = st.enter_context((self.nc.psum_tensor if psum else self.nc.sbuf_tensor)(self.nm(name), list(shape), dt))
        return T(t, self.key(name))

    def nm(self, name):
        self.uid += 1
        return "%s_u%d" % (name, self.uid)

    def dram(self, name, shape, dt, kind="Internal"):
        ap = self.nc.dram_tensor(name, list(shape), dt, kind=kind).ap()
        self.scr[name] = ap
        return ap

    def barrier(self):
        fw = self.fw
        toks = [(e, fw.cnt[e]) for e in fw.CE if fw.cnt[e] > 0]
        toks += [(b, b.base + 16 * b.n) for b in self.keys if b.n > 0 and b.sem is not None]
        for e in ("pe", "act", "dve", "pool", "sp"):
            fw.barrier_wait(e, toks)
        self.marks.append(dict(fw.cnt))
        dead = [b for b in self.keys if not b.persist]
        for b in dead:
            fw.release(b)
            b.dead = True
        self.dead.extend(dead)
        self.keys = [b for b in self.keys if b.persist]

    def prep_w(self, name, W, cw=512, kcg_max=16, defer=False):
        K, N = W.shape
        KC = K // 128
        KG = (KC + kcg_max - 1) // kcg_max
        kcg = KC // KG
        assert KG * kcg == KC and N % cw == 0
        NT = N // cw
        ws = self.dram("ws_" + name, [NT, KG, 128, kcg, cw], BF16)
        src_b = self.db("in_" + name)
        tiles = {}
        jobs = []
        for ct in range(NT):
            for kg in range(KG):
                src = W[kg * kcg * 128:(kg + 1) * kcg * 128, ct * cw:(ct + 1) * cw].rearrange("(kc p) n -> p kc n", p=128)
                tb = Buf("w_%s_%d_%d" % (name, ct, kg))
                tiles[(ct, kg)] = tb
                jobs.append(lambda dst=ws[ct, kg], src=src, tb=tb: self.cast_dma(dst, src, src_b, tb))
        self.wts[name] = (ws, tiles, kcg, KG, NT)
        if defer:
            self.cast_q.extend(jobs)
        else:
            for j in jobs:
                j()

    def cast_dma(self, dst, src, src_b, tb, same_key=None):
        if not self.cast_keys:
            old_p, self.persist = self.persist, True
            self.cast_keys = [self.key("castk%d" % i) for i in range(4)]
            self.persist = old_p
        key = same_key if same_key is not None else self.cast_keys[self.cast_i % 4]
        if same_key is None:
            self.cast_i += 1
        if key.n > 0:
            self.fw._wait("pool", {key: key.base + 16 * key.n})
        self.fw.dma("pool", dst, src, r=[src_b], w=[tb], key=key)
        return key

    def pump(self, n=1):
        while n > 0 and self.cast_q:
            self.cast_q.pop(0)()
            n -= 1

    def flush_casts(self):
        self.pump(len(self.cast_q))

    def wload(self, name, ct, kg=0):
        ws, tiles, kcg, KG, NT = self.wts[name]
        slot = self.wring[self.wslot_i % len(self.wring)]
        self.wslot_i += 1
        cw = ws.shape[-1]
        view = slot.t[:, 0:kcg * cw].rearrange("p (k n) -> p k n", n=cw)
        self.fw.dma("sp", view, ws[ct, kg], r=[tiles[(ct, kg)]], w=[slot.b], key=slot.b)
        return T(view, slot.b)

    def setup_consts(self, st):
        fw = self.fw
        self.identf = self.tile(st, "identf", [128, 128], F32)
        self.ident = self.tile(st, "ident", [128, 128], BF16)
        fw.op("pool", lambda E: E.memset(self.identf.t[:], 1.0), w=[self.identf.b])
        fw.op("pool", lambda E: E.affine_select(out=self.identf.t[:], in_=self.identf.t[:], pattern=[[1, 128]],
                                                compare_op=ALU.is_equal, fill=0.0, base=0, channel_multiplier=-1),
              r=[self.identf.b], w=[self.identf.b])
        fw.op("dve", lambda E: E.tensor_copy(out=self.ident.t[:], in_=self.identf.t[:]), r=[self.identf.b], w=[self.ident.b])
        self.mhalf = self.tile(st, "mhalf", [128, 1], F32)
        fw.op("pool", lambda E: E.memset(self.mhalf.t[:], -0.5), w=[self.mhalf.b])
        self.wring = [self.tile(st, "wring%d" % i, [128, 16 * 512], BF16) for i in range(4)]
        self.persist = False

    def bcast_row(self, st, name, row_ap, n):
        t = self.tile(st, name, [128, n], F32)
        self.fw.dma("sp", t.t[:], row_ap.partition_broadcast(128), r=[self.db("in_" + name)], w=[t.b])
        return t

    def rmsnorm(self, xin, xb, Dn, gb, out, outb, junk, stat):
        fw = self.fw
        ss, ms, rstd = stat
        fw.op("dve", lambda E: E.scalar_tensor_tensor(out=junk.t[:, 0:Dn], in0=xin, scalar=1.0, in1=xin,
                                                      op0=ALU.mult, op1=ALU.mult, accum_out=ss.t[:]),
              r=[xb], w=[junk.b, ss.b])
        fw.op("dve", lambda E: E.tensor_scalar(out=ms.t[:], in0=ss.t[:], scalar1=1.0 / Dn, scalar2=EPS,
                                               op0=ALU.mult, op1=ALU.add), r=[ss.b], w=[ms.b])
        fw.op("act", lambda E: E.activation(out=ms.t[:], in_=ms.t[:], func=AF.Sqrt), r=[ms.b], w=[ms.b])
        fw.op("dve", lambda E: E.reciprocal(out=rstd.t[:], in_=ms.t[:]), r=[ms.b], w=[rstd.b])
        fw.op("dve", lambda E: E.scalar_tensor_tensor(out=out, in0=xin, scalar=rstd.t[:], in1=gb.t[:, 0:Dn],
                                                      op0=ALU.mult, op1=ALU.mult),
              r=[xb, rstd.b, gb.b], w=[outb])

    def transpose_to(self, src, srcb, nch, dst, dstb_list, ch0, col0, pts, flip=0):
        fw = self.fw
        if not isinstance(pts, (list, tuple)):
            pts = [pts]
        for g0 in range(0, nch, 8):
            g = min(8, nch - g0)
            pt = pts[(g0 // 8 + flip) % len(pts)]
            for j in range(g):
                fw.op("pe", lambda E: E.transpose(out=pt.t[:, j, :], in_=src[:, (g0 + j) * 128:(g0 + j + 1) * 128],
                                                  identity=self.ident.t[:]),
                      r=[srcb, self.ident.b], w=[pt.b])
            eng = "act" if ((g0 // 8 + flip) % 2 == 0) else "dve"
            o = dst[:, ch0 + g0:ch0 + g0 + g, col0:col0 + 128]
            i = pt.t[:, 0:g, :]
            if eng == "act":
                fw.op("act", lambda E: E.copy(out=o, in_=i), r=[pt.b], w=[dstb_list[g0 // 8]])
            else:
                fw.op("dve", lambda E: E.tensor_copy(out=o, in_=i), r=[pt.b], w=[dstb_list[g0 // 8]])

    def ffn(self, L, xin, xin_name, xout, xout_name, gain_row):
        fw = self.fw
        S = self.S
        wg, wu, wd = "wg%d" % L, "wu%d" % L, "wd%d" % L
        with contextlib.ExitStack() as st:
            gb = self.bcast_row(st, "ffn_g%d" % L, gain_row, D)
            xs = [self.tile(st, "f_xs%d" % i, [128, D], F32) for i in range(2)]
            hb = [self.tile(st, "f_hb%d" % i, [128, D], BF16) for i in range(2)]
            junk = self.tile(st, "f_junk", [128, D], BF16)
            stats = [[self.tile(st, "f_st%d_%d" % (i, j), [128, 1], F32) for j in range(3)] for i in range(2)]
            hTs = [st.enter_context(self.nc.sbuf_tensor(self.nm("f_hT%d" % i), [128, 16, TT], BF16)) for i in range(2)]
            hTbs = [[[self.key("f_hTb%d_%d_%d" % (i, s, h)) for h in range(2)] for s in range(NSUB)] for i in range(2)]
            aT = st.enter_context(self.nc.sbuf_tensor(self.nm("f_aT"), [128, FH // 128, TT], BF16))
            aTb = [self.key("f_aTb%d" % i) for i in range(FH // 128)]
            sg = [self.tile(st, "f_sg%d" % i, [128, TT], F32) for i in range(2)]
            xr = [self.tile(st, "f_xr%d" % i, [128, 512], F32) for i in range(4)]
            xo = [self.tile(st, "f_xo%d" % i, [128, 512], F32) for i in range(4)]
            pt = [self.tile(st, "f_pt%d" % i, [128, 8, 128], BF16, psum=True) for i in range(2)]
            pg = self.tile(st, "f_pg", [128, TT], F32, psum=True)
            pu = self.tile(st, "f_pu", [128, TT], F32, psum=True)
            py = [self.tile(st, "f_py%d" % i, [128, 512], F32, psum=True) for i in range(NSUB)]
            nfc = FH // 128
            def norm_tile(tt):
                for s in range(NSUB):
                    row = tt * NSUB + s
                    x_ = xs[row % 2]
                    h_ = hb[row % 2]
                    fw.dma("sp", x_.t[:], xin[row * 128:(row + 1) * 128, :], r=[self.db(xin_name, row)], w=[x_.b])
                    self.rmsnorm(x_.t[:], x_.b, D, gb, h_.t[:], h_.b, junk, stats[row % 2])
                    self.transpose_to(h_.t, h_.b, 16, hTs[tt % 2], hTbs[tt % 2][s], 0, s * 128, pt, flip=s)

            norm_tile(0)
            for tt in range(S // TT):
                hT = hTs[tt % 2]
                hTb = hTbs[tt % 2]
                hT_all = [b for l in hTb for b in l]
                for ft in range(FH // 512):
                    wgt = self.wload(wg, ft)
                    wut = self.wload(wu, ft)
                    for j in range(4):
                        fc = ft * 4 + j
                        for kc in range(16):
                            fw.op("pe", lambda E: E.matmul(pg.t[:], lhsT=wgt.t[:, kc, j * 128:(j + 1) * 128], rhs=hT[:, kc, :],
                                                           start=(kc == 0), stop=(kc == 15)),
                                  r=[wgt.b] + hT_all, w=[pg.b])
                        for kc in range(16):
                            fw.op("pe", lambda E: E.matmul(pu.t[:], lhsT=wut.t[:, kc, j * 128:(j + 1) * 128], rhs=hT[:, kc, :],
                                                           start=(kc == 0), stop=(kc == 15)),
                                  r=[wut.b] + hT_all, w=[pu.b])
                        sg_ = sg[fc % 2]
                        fw.op("act", lambda E: E.activation(out=sg_.t[:], in_=pg.t[:], func=AF.Silu), r=[pg.b], w=[sg_.b])
                        fw.op("dve", lambda E: E.tensor_tensor(out=aT[:, fc, :], in0=sg_.t[:], in1=pu.t[:], op=ALU.mult),
                              r=[sg_.b, pu.b], w=[aTb[fc]])
                if tt + 1 < S // TT:
                    norm_tile(tt + 1)
                for dt in range(D // 512):
                    for kg in range(4):
                        wdt = self.wload(wd, dt, kg)
                        for s in range(NSUB):
                            for k in range(11):
                                fc = kg * 11 + k
                                fw.op("pe", lambda E: E.matmul(py[s].t[:], lhsT=aT[:, fc, s * 128:(s + 1) * 128], rhs=wdt.t[:, k, :],
                                                               start=(fc == 0), stop=(fc == nfc - 1)),
                                      r=[wdt.b, aTb[fc]], w=[py[s].b])
                    for s in range(NSUB):
                        row = tt * NSUB + s
                        fw.dma("sp", xr[s].t[:], xin[row * 128:(row + 1) * 128, dt * 512:(dt + 1) * 512],
                               r=[self.db(xin_name, row)], w=[xr[s].b])
                        fw.op("dve", lambda E: E.tensor_tensor(out=xo[s].t[:], in0=py[s].t[:], in1=xr[s].t[:], op=ALU.add),
                              r=[py[s].b, xr[s].b], w=[xo[s].b])
                        fw.dma("act", xout[row * 128:(row + 1) * 128, dt * 512:(dt + 1) * 512], xo[s].t[:],
                               r=[xo[s].b], w=[self.db(xout_name, row)], key=xo[s].b)
            self.barrier()

    def final_norm(self, xin, xin_name, xout, xout_name, gain_row):
        fw = self.fw
        with contextlib.ExitStack() as st:
            gb = self.bcast_row(st, "fin_g", gain_row, D)
            xs = [self.tile(st, "n_xs%d" % i, [128, D], F32) for i in range(3)]
            os_ = [self.tile(st, "n_os%d" % i, [128, D], F32) for i in range(3)]
            junk = self.tile(st, "n_junk", [128, D], BF16)
            stats = [[self.tile(st, "n_st%d_%d" % (i, j), [128, 1], F32) for j in range(3)] for i in range(3)]
            toks = []
            for row in range(self.S // 128):
                x_ = xs[row % 3]
                o_ = os_[row % 3]
                fw.dma("sp", x_.t[:], xin[row * 128:(row + 1) * 128, :], r=[self.db(xin_name, row)], w=[x_.b])
                self.rmsnorm(x_.t[:], x_.b, D, gb, o_.t[:], o_.b, junk, stats[row % 3])
                toks.append(fw.dma("act", xout[row * 128:(row + 1) * 128, :], o_.t[:], r=[o_.b], w=[self.db(xout_name, row)], key=o_.b))
            self.barrier()

    def norm_T(self, st_tiles, xin, xin_name, tt, gb, hT, hTb, pt):
        xs, hb, junk, stats = st_tiles
        fw = self.fw
        for s in range(NSUB):
            row = tt * NSUB + s
            x_ = xs[row % 2]
            h_ = hb[row % 2]
            fw.dma("sp", x_.t[:], xin[row * 128:(row + 1) * 128, :], r=[self.db(xin_name, row)], w=[x_.b])
            self.rmsnorm(x_.t[:], x_.b, D, gb, h_.t[:], h_.b, junk, stats[row % 2])
            self.transpose_to(h_.t, h_.b, 16, hT, hTb[s], 0, s * 128, pt, flip=s)

    def norm_tiles(self, st, pfx):
        xs = [self.tile(st, pfx + "_xs%d" % i, [128, D], F32) for i in range(2)]
        hb = [self.tile(st, pfx + "_hb%d" % i, [128, D], BF16) for i in range(2)]
        junk = self.tile(st, pfx + "_junk", [128, D], BF16)
        stats = [[self.tile(st, pfx + "_st%d_%d" % (i, j), [128, 1], F32) for j in range(3)] for i in range(2)]
        return xs, hb, junk, stats

    def proj_resid(self, tt, aT, aTb, wname, KG, kcg, py, xr, xo, xin, xin_name, xout, xout_name):
        fw = self.fw
        nkc = KG * kcg
        for dt in range(D // 512):
            for kg in range(KG):
                wdt = self.wload(wname, dt, kg)
                for s in range(NSUB):
                    for k in range(kcg):
                        fc = kg * kcg + k
                        fw.op("pe", lambda E: E.matmul(py[s].t[:], lhsT=aT[:, fc, s * 128:(s + 1) * 128], rhs=wdt.t[:, k, :],
                                                       start=(fc == 0), stop=(fc == nkc - 1)),
                              r=[wdt.b, aTb[fc]], w=[py[s].b])
            for s in range(NSUB):
                row = tt * NSUB + s
                fw.dma("sp", xr[s].t[:], xin[row * 128:(row + 1) * 128, dt * 512:(dt + 1) * 512],
                       r=[self.db(xin_name, row)], w=[xr[s].b])
                fw.op("dve", lambda E: E.tensor_tensor(out=xo[s].t[:], in0=py[s].t[:], in1=xr[s].t[:], op=ALU.add),
                      r=[py[s].b, xr[s].b], w=[xo[s].b])
                fw.dma("act", xout[row * 128:(row + 1) * 128, dt * 512:(dt + 1) * 512], xo[s].t[:],
                       r=[xo[s].b], w=[self.db(xout_name, row)], key=xo[s].b)

    def out_proj(self, wname, KC, ymT, ym_name, xin, xin_name, xout, xout_name):
        fw = self.fw
        KG = max(1, KC // 16)
        kcg = KC // KG
        with contextlib.ExitStack() as st:
            yT = [st.enter_context(self.nc.sbuf_tensor(self.nm("o_yT%d" % i), [128, KC, TT], BF16)) for i in range(2)]
            yTb = [[self.key("o_yTb%d_%d" % (i, k)) for k in range(KC)] for i in range(2)]
            xr = [self.tile(st, "o_xr%d" % i, [128, 512], F32) for i in range(4)]
            xo = [self.tile(st, "o_xo%d" % i, [128, 512], F32) for i in range(4)]
            py = [self.tile(st, "o_py%d" % i, [128, 512], F32, psum=True) for i in range(NSUB)]
            for tt in range(self.S // TT):
                y_ = yT[tt % 2]
                yb_ = yTb[tt % 2]
                for k in range(KC):
                    fw.dma("sp", y_[:, k, :], ymT[k * 128:(k + 1) * 128, tt * TT:(tt + 1) * TT],
                           r=[self.db(ym_name, (k, tt))], w=[yb_[k]])
                self.proj_resid(tt, y_, yb_, wname, KG, kcg, py, xr, xo, xin, xin_name, xout, xout_name)
            self.barrier()

    def sb_qkv(self, xin, xin_name, gain_row, qT, kT, vv, qscale):
        fw = self.fw
        with contextlib.ExitStack() as st:
            gb = self.bcast_row(st, "sbq_g", gain_row, D)
            nt = self.norm_tiles(st, "sbq")
            hTs = [st.enter_context(self.nc.sbuf_tensor(self.nm("sbq_hT%d" % i), [128, 16, TT], BF16)) for i in range(2)]
            hTbs = [[[self.key("sbq_hTb%d_%d_%d" % (i, s, h)) for h in range(2)] for s in range(NSUB)] for i in range(2)]
            ev = [self.tile(st, "sbq_ev%d" % i, [128, 512], BF16) for i in range(8)]
            pt = [self.tile(st, "sbq_pt%d" % i, [128, 8, 128], BF16, psum=True) for i in range(2)]
            pp = [self.tile(st, "sbq_pp%d" % i, [128, 512], F32, psum=True) for i in range(4)]
            n = 0
            self.norm_T(nt, xin, xin_name, 0, gb, hTs[0], hTbs[0], pt)
            for tt in range(self.S // TT):
                hT = hTs[tt % 2]
                hTb = hTbs[tt % 2]
                hT_all = [b for l in hTb for b in l]
                for ct in range(12):
                    if ct == 8 and tt + 1 < self.S // TT:
                        self.norm_T(nt, xin, xin_name, tt + 1, gb, hTs[(tt + 1) % 2], hTbs[(tt + 1) % 2], pt)
                    wt = self.wload("wqkv", ct)
                    for j in range(4):
                        p_ = pp[n % 4]
                        e_ = ev[n % 8]
                        if ct < 8:
                            for kc in range(16):
                                fw.op("pe", lambda E: E.matmul(p_.t[:], lhsT=wt.t[:, kc, j * 128:(j + 1) * 128], rhs=hT[:, kc, :],
                                                               start=(kc == 0), stop=(kc == 15)), r=[wt.b] + hT_all, w=[p_.b])
                            h = (ct % 4) * 4 + j
                            dst = (qT if ct < 4 else kT)[h, :, tt * TT:(tt + 1) * TT]
                            dname = ("qT" if ct < 4 else "kT", (h, tt))
                        else:
                            for kc in range(16):
                                fw.op("pe", lambda E: E.matmul(p_.t[:], lhsT=hT[:, kc, j * 128:(j + 1) * 128], rhs=wt.t[:, kc, :],
                                                               start=(kc == 0), stop=(kc == 15)), r=[wt.b] + hT_all, w=[p_.b])
                            row = tt * NSUB + j
                            h0 = (ct - 8) * 4
                            dst = vv[h0:h0 + 4, row * 128:(row + 1) * 128, :].rearrange("h t d -> t h d")
                            dname = ("vv", (ct - 8, row))
                        if ct < 4:
                            if n % 2 == 0:
                                fw.op("act", lambda E: E.mul(out=e_.t[:], in_=p_.t[:], mul=qscale), r=[p_.b], w=[e_.b])
                            else:
                                fw.op("dve", lambda E: E.tensor_scalar(out=e_.t[:], in0=p_.t[:], scalar1=qscale, scalar2=None, op0=ALU.mult),
                                      r=[p_.b], w=[e_.b])
                        elif n % 2 == 0:
                            fw.op("act", lambda E: E.copy(out=e_.t[:], in_=p_.t[:]), r=[p_.b], w=[e_.b])
                        else:
                            fw.op("dve", lambda E: E.tensor_copy(out=e_.t[:], in_=p_.t[:]), r=[p_.b], w=[e_.b])
                        src = e_.t[:] if ct < 8 else e_.t[:].rearrange("t (h d) -> t h d", h=4)
                        fw.dma("act", dst, src, r=[e_.b], w=[self.db(*dname)], key=e_.b)
                        n += 1
            self.barrier()

    def attn_consts(self, st):
        fw = self.fw
        c = {}

        def mk(name, val, pattern, cm, cmp_):
            f = self.tile(st, "ac_f_" + name, [128, 128], F32)
            b = self.tile(st, "ac_b_" + name, [128, 128], BF16)
            fw.op("pool", lambda E: E.memset(f.t[:], val), w=[f.b])
            fw.op("pool", lambda E: E.affine_select(out=f.t[:], in_=f.t[:], pattern=pattern, compare_op=cmp_, fill=0.0,
                                                    base=0, channel_multiplier=cm), r=[f.b], w=[f.b])
            fw.op("dve", lambda E: E.tensor_copy(out=b.t[:], in_=f.t[:]), r=[f.b], w=[b.b])
            c[name] = (f, b)

        mk("triIN", -1.0, [[-1, 128]], 1, ALU.is_ge)
        mk("triSN", -1.0, [[1, 128]], -1, ALU.is_gt)
        mk("mS", 1.0, [[1, 128]], -1, ALU.is_gt)
        mk("mC", 1.0, [[1, 128]], -1, ALU.is_ge)
        ones = self.tile(st, "ac_ones", [128, 128], BF16)
        zeros = self.tile(st, "ac_zeros", [128, 512], BF16)
        fw.op("pool", lambda E: E.memset(ones.t[:], 1.0), w=[ones.b])
        fw.op("pool", lambda E: E.memset(zeros.t[:], 0.0), w=[zeros.b])
        c["ones"] = ones
        c["zeros"] = zeros
        return c

    def attention(self, kind, nheads, qT, kT, vv, ymT, ym_name, row0, scale, qpT=None, kpT=None):
        fw = self.fw
        S = self.S
        NQT = S // TT
        NCH = S // 128
        NS = 4 if kind == "mla" else 2
        NKV = 1 if kind == "mla" else 2
        NPS = 1 if kind == "mla" else 2
        NB = 3
        LEAD = 1 if kind == "mla" else 2
        with contextlib.ExitStack() as st:
            C = self.attn_consts(st)
            onesf = self.tile(st, "at_onesf", [128, 128], F32)
            fw.op("pool", lambda E: E.memset(onesf.t[:], 1.0), w=[onesf.b])
            kpe = None
            if kind == "mla":
                kpe = self.tile(st, "at_kpe", [128, S], BF16)
                fw.op("pool", lambda E: E.memset(kpe.t[64:128, :], 0.0), w=[kpe.b])
                fw.dma("sp", kpe.t[0:64, :], kpT, r=[self.db("kpT")], w=[kpe.b])
            streams = []
            for si in range(NS):
                R = {}
                R["k"] = [self.tile(st, "at_k%d_%d" % (si, i), [128, S], BF16) for i in range(NKV)]
                R["v"] = [self.tile(st, "at_v%d_%d" % (si, i), [128, NCH, 128], BF16) for i in range(NKV)]
                R["q"] = [self.tile(st, "at_q%d_%d" % (si, i), [128, TT], BF16) for i in range(2)]
                if kind == "mla":
                    R["qp"] = [self.tile(st, "at_qp%d_%d" % (si, i), [128, TT], BF16) for i in range(2)]
                    for qp__ in R["qp"]:
                        fw.op("pool", lambda E: E.memset(qp__.t[64:128, :], 0.0), w=[qp__.b])
                    R["P"] = [self.tile(st, "at_P%d_%d" % (si, i), [128, TT], BF16) for i in range(3)]
                    R["acc"] = self.tile(st, "at_acc%d" % si, [128, TT], F32)
                    R["rs"] = self.tile(st, "at_rs%d" % si, [128, TT], F32)
                else:
                    for nm, dt_ in (("e", F32), ("sp", F32), ("hi", BF16), ("lo", BF16), ("P", BF16)):
                        R[nm] = [self.tile(st, "at_%s%d_%d" % (nm, si, i), [128, TT], dt_) for i in range(NB if nm in ("e", "sp", "hi", "lo") else 2)]
                    R["kn"] = [self.tile(st, "at_kn%d_%d" % (si, i), [128, S], BF16) for i in range(1)]
                R["o"] = [self.tile(st, "at_o%d_%d" % (si, i), [128, TT], BF16) for i in range(2)]
                R["ps"] = [self.tile(st, "at_ps%d_%d" % (si, i), [128, TT], F32, psum=True) for i in range(NPS)]
                R["A"] = self.tile(st, "at_A%d" % si, [128, TT], F32, psum=True) if kind == "sb" else R["ps"][0]
                R["O"] = self.tile(st, "at_O%d" % si, [128, TT], F32, psum=True)
                streams.append(R)

            def geom(c, qt):
                j = c - 4 * qt
                q0 = j * 128 if j > 0 else 0
                return q0, (j >= 0)

            def stream(si):
                R = streams[si]
                hn = 0
                cnt = 0
                for h in range(si, nheads, NS):
                    k_ = R["k"][hn % NKV]
                    v_ = R["v"][hn % NKV]
                    hn += 1
                    fw.dma("sp", k_.t[:], kT[h], r=[self.db("kT_all")], w=[k_.b])
                    fw.dma("sp", v_.t[:], vv[h].rearrange("(c p) d -> p c d", p=128), r=[self.db("vv_all")], w=[v_.b])
                    kn_ = None
                    if kind == "sb":
                        kn_ = R["kn"][0]
                        fw.op("pool", lambda E: E.tensor_scalar(out=kn_.t[:], in0=k_.t[:], scalar1=-1.0, scalar2=0.0, op0=ALU.mult, op1=ALU.add),
                              r=[k_.b], w=[kn_.b])
                    for qt in range(NQT):
                        self.pump(1)
                        q_ = R["q"][qt % 2]
                        fw.dma("sp", q_.t[:], qT[h, :, qt * TT:(qt + 1) * TT], r=[self.db("qT_all")], w=[q_.b])
                        qp_ = None
                        if kind == "mla":
                            qp_ = R["qp"][qt % 2]
                            fw.dma("sp", qp_.t[0:64, :], qpT[h, :, qt * TT:(qt + 1) * TT], r=[self.db("qpT_all")], w=[qp_.b])
                        A = R["A"]
                        O = R["O"]
                        fw.op("pe", lambda E: E.matmul(O.t[:], lhsT=C["zeros"].t[:, 0:128], rhs=C["zeros"].t[:], start=True, stop=False),
                              r=[C["zeros"].b], w=[O.b])
                        if kind == "sb":
                            fw.op("pe", lambda E: E.matmul(A.t[:], lhsT=C["zeros"].t[:, 0:128], rhs=C["zeros"].t[:], start=True, stop=False),
                                  r=[C["zeros"].b], w=[A.b])
                        else:
                            acc = R["acc"]
                            fw.op("pool", lambda E: E.memset(acc.t[:], 0.0), w=[acc.b])
                        nk = 4 * (qt + 1)
                        order = list(range(nk - 1, -1, -1))

                        def stageA(c, slot):
                            q0, diag = geom(c, qt)
                            ps = R["ps"][slot % NPS]
                            kc_ap = k_.t[:, c * 128:(c + 1) * 128]
                            if kind == "mla":
                                P = R["P"][slot % 3]
                                fw.op("pe", lambda E: E.matmul(ps.t[:, q0:], lhsT=kc_ap, rhs=q_.t[:, q0:], start=True, stop=False),
                                      r=[k_.b, q_.b], w=[ps.b])
                                fw.op("pe", lambda E: E.matmul(ps.t[:, q0:], lhsT=kpe.t[:, c * 128:(c + 1) * 128], rhs=qp_.t[:, q0:],
                                                               start=False, stop=True), r=[kpe.b, qp_.b], w=[ps.b])
                                fw.op("act", lambda E: E.activation(out=P.t[:, q0:], in_=ps.t[:, q0:], func=AF.Exp, scale=scale),
                                      r=[ps.b], w=[P.b])
                                if diag:
                                    fw.op("pool", lambda E: E.tensor_tensor(out=P.t[:, q0:q0 + 128], in0=P.t[:, q0:q0 + 128],
                                                                            in1=C["mC"][1].t[:], op=ALU.mult),
                                          r=[P.b, C["mC"][1].b], w=[P.b])
                                fw.op("dve", lambda E: E.tensor_tensor(out=acc.t[:, q0:], in0=acc.t[:, q0:], in1=P.t[:, q0:], op=ALU.add),
                                      r=[acc.b, P.b], w=[acc.b])
                                return
                            i3 = slot % NB
                            e_, sp_, hi_, lo_ = (R[n_][i3] for n_ in ("e", "sp", "hi", "lo"))
                            fw.op("pe", lambda E: E.matmul(ps.t[:, q0:], lhsT=kc_ap, rhs=q_.t[:, q0:], start=True, stop=True),
                                  r=[k_.b, q_.b], w=[ps.b])
                            fw.op("act", lambda E: E.activation(out=e_.t[:, q0:], in_=ps.t[:, q0:], func=AF.Exp, scale=scale),
                                  r=[ps.b], w=[e_.b])
                            fw.op("act", lambda E: E.activation(out=sp_.t[:, q0:], in_=e_.t[:, q0:], func=AF.Ln, bias=1.0),
                                  r=[e_.b], w=[sp_.b])
                            if diag:
                                fw.op("pool", lambda E: E.tensor_tensor(out=sp_.t[:, q0:q0 + 128], in0=sp_.t[:, q0:q0 + 128],
                                                                        in1=C["mS"][0].t[:], op=ALU.mult),
                                      r=[sp_.b, C["mS"][0].b], w=[sp_.b])
                            fw.op("dve", lambda E: E.tensor_copy(out=hi_.t[:, q0:], in_=sp_.t[:, q0:]), r=[sp_.b], w=[hi_.b])
                            fw.op("dve", lambda E: E.tensor_tensor(out=lo_.t[:, q0:], in0=sp_.t[:, q0:], in1=hi_.t[:, q0:], op=ALU.subtract),
                                  r=[sp_.b, hi_.b], w=[lo_.b])

                        def stageB(c, slot):
                            q0, diag = geom(c, qt)
                            last = (c == 0)
                            if kind == "mla":
                                P = R["P"][slot % 3]
                                fw.op("pe", lambda E: E.matmul(O.t[:, q0:], lhsT=v_.t[:, c, :], rhs=P.t[:, q0:], start=False, stop=last),
                                      r=[v_.b, P.b], w=[O.b])
                                return
                            i2 = slot % 2
                            i3 = slot % NB
                            hi_, lo_ = (R[n_][i3] for n_ in ("hi", "lo"))
                            P = R["P"][i2]
                            for x_ in (hi_, lo_):
                                fw.op("pe", lambda E: E.matmul(A.t[:, q0:], lhsT=C["triIN"][1].t[:], rhs=x_.t[:, q0:], start=False, stop=False),
                                      r=[C["triIN"][1].b, x_.b], w=[A.b])
                            fw.op("pe", lambda E: E.matmul(A.t[:, q0:], lhsT=k_.t[:, c * 128:(c + 1) * 128], rhs=q_.t[:, q0:], start=False, stop=False),
                                  r=[k_.b, q_.b], w=[A.b])
                            fw.op("act", lambda E: E.activation(out=P.t[:, q0:], in_=A.t[:, q0:], func=AF.Exp), r=[A.b], w=[P.b])
                            if diag:
                                fw.op("pool", lambda E: E.tensor_tensor(out=P.t[:, q0:q0 + 128], in0=P.t[:, q0:q0 + 128],
                                                                        in1=C["mS"][1].t[:], op=ALU.mult),
                                      r=[P.b, C["mS"][1].b], w=[P.b])
                            yield_here.append(1)

                        def stageB2(c, slot):
                            q0, diag = geom(c, qt)
                            last = (c == 0)
                            i3 = slot % NB
                            hi_, lo_ = (R[n_][i3] for n_ in ("hi", "lo"))
                            if not last:
                                for x_ in (hi_, lo_):
                                    fw.op("pe", lambda E: E.matmul(A.t[:, q0:], lhsT=C["triSN"][1].t[:], rhs=x_.t[:, q0:], start=False, stop=False),
                                          r=[C["triSN"][1].b, x_.b], w=[A.b])
                                fw.op("pe", lambda E: E.matmul(A.t[:, q0:], lhsT=kn_.t[:, c * 128:(c + 1) * 128], rhs=q_.t[:, q0:], start=False, stop=False),
                                      r=[kn_.b, q_.b], w=[A.b])

                        def pv(c, slot):
                            q0, diag = geom(c, qt)
                            P = R["P"][slot % 2]
                            fw.op("pe", lambda E: E.matmul(O.t[:, q0:], lhsT=v_.t[:, c, :], rhs=P.t[:, q0:], start=False, stop=(c == 0)),
                                  r=[v_.b, P.b], w=[O.b])

                        yield_here = []
                        for l_ in range(min(LEAD, len(order))):
                            stageA(order[l_], cnt + l_)
                        for idx, c in enumerate(order):
                            slot = cnt + idx
                            if idx + LEAD < len(order):
                                stageA(order[idx + LEAD], slot + LEAD)
                            yield
                            if kind == "mla":
                                stageB(c, slot)
                            else:
                                stageB(c, slot)
                                yield
                                stageB2(c, slot)
                                if idx > 0:
                                    pv(order[idx - 1], slot - 1)
                        if kind == "sb":
                            pv(order[-1], cnt + len(order) - 1)
                        cnt += len(order)
                        o_ = R["o"][qt % 2]
                        if kind == "mla":
                            rs = R["rs"]
                            fw.op("pe", lambda E: E.matmul(A.t[:], lhsT=onesf.t[:], rhs=acc.t[:], start=True, stop=True),
                                  r=[onesf.b, acc.b], w=[A.b])
                            fw.op("dve", lambda E: E.reciprocal(out=rs.t[:], in_=A.t[:]), r=[A.b], w=[rs.b])
                            fw.op("dve", lambda E: E.tensor_tensor(out=o_.t[:], in0=O.t[:], in1=rs.t[:], op=ALU.mult),
                                  r=[O.b, rs.b], w=[o_.b])
                        else:
                            fw.op("act", lambda E: E.copy(out=o_.t[:], in_=O.t[:]), r=[O.b], w=[o_.b])
                        fw.dma("act", ymT[row0 + h * 128:row0 + (h + 1) * 128, qt * TT:(qt + 1) * TT], o_.t[:],
                               r=[o_.b], w=[self.db(ym_name, (row0 // 128 + h, qt))], key=o_.b)
                        yield

            gens = [stream(i) for i in range(NS)]
            alive = [True] * NS
            while any(alive):
                for i in range(NS):
                    if alive[i]:
                        try:
                            next(gens[i])
                        except StopIteration:
                            alive[i] = False
            self.barrier()

    def l0_inproj(self, xin, xin_name, P, O):
        fw = self.fw
        S = self.S
        with contextlib.ExitStack() as st:
            gb = self.bcast_row(st, "l0_g", P["mix_norm"], D)
            gq = self.bcast_row(st, "l0_gq", P["q_norm"], 512)
            gkv = self.bcast_row(st, "l0_gkv", P["kv_norm"], 512)
            dtb = self.bcast_row(st, "l0_dtb", P["dt_bias"], 32)
            ab = self.bcast_row(st, "l0_ab", P["a_log"], 32)
            fw.op("act", lambda E: E.activation(out=ab.t[:], in_=ab.t[:], func=AF.Exp), r=[ab.b], w=[ab.b])
            fw.op("dve", lambda E: E.tensor_scalar(out=ab.t[:], in0=ab.t[:], scalar1=-1.0, scalar2=None, op0=ALU.mult),
                  r=[ab.b], w=[ab.b])
            nt = self.norm_tiles(st, "l0")
            hTs = [st.enter_context(self.nc.sbuf_tensor(self.nm("l0_hT%d" % i), [128, 16, TT], BF16)) for i in range(2)]
            hTbs = [[[self.key("l0_hTb%d_%d_%d" % (i, s, h)) for h in range(2)] for s in range(NSUB)] for i in range(2)]
            pt = [self.tile(st, "l0_pt%d" % i, [128, 8, 128], BF16, psum=True) for i in range(2)]
            pp = [self.tile(st, "l0_pp%d" % i, [128, 512], F32, psum=True) for i in range(6)]
            pA = pp[0]
            evf = [self.tile(st, "l0_evf%d" % i, [128, 512], F32) for i in range(5)]
            evb = [self.tile(st, "l0_evb%d" % i, [128, 512], BF16) for i in range(6)]
            wsm = self.tile(st, "l0_wsm", [128, 16, 96], BF16)
            ws_, tiles_, _, _, _ = self.wts["wsm"]
            fw.dma("sp", wsm.t[:], ws_[0, 0], r=[tiles_[(0, 0)]], w=[wsm.b])
            cw_raw = self.tile(st, "l0_cwraw", [120, 128], F32)
            cwT = self.tile(st, "l0_cwT", [128, 120], F32)
            fw.dma("sp", cw_raw.t[0:96, :], P["conv_w"].rearrange("i (c p) -> (i c) p", p=128), r=[self.db("in_conv_w")], w=[cw_raw.b])
            fw.dma("sp", cw_raw.t[96:120, :], P["conv_b"].rearrange("(c p) -> c p", p=128), r=[self.db("in_conv_b")], w=[cw_raw.b])
            fw.op("pe", lambda E: E.transpose(out=pA.t[:, 0:120], in_=cw_raw.t[:], identity=self.identf.t[0:120, 0:120]),
                  r=[cw_raw.b, self.identf.b], w=[pA.b])
            fw.op("dve", lambda E: E.tensor_copy(out=cwT.t[:], in_=pA.t[:, 0:120]), r=[pA.b], w=[cwT.b])
            perm = self.tile(st, "l0_perm", [128, 128], F32)
            fw.op("pool", lambda E: E.memset(perm.t[:], 0.0), w=[perm.b])
            for (c0_, base_, fill_) in ((0, -32, -1.0), (32, 0, 1.0), (64, -96, -1.0), (96, -64, 1.0)):
                fw.op("pool", lambda E: E.affine_select(out=perm.t[:, c0_:c0_ + 32], in_=perm.t[:, c0_:c0_ + 32], pattern=[[-1, 32]],
                                                        compare_op=ALU.not_equal, fill=fill_, base=base_, channel_multiplier=1),
                      r=[perm.b], w=[perm.b])
            work = [self.tile(st, "l0_work%d" % i, [128, 515], F32) for i in range(2)]
            acc = [self.tile(st, "l0_acc%d" % i, [128, 512], F32) for i in range(2)]
            carry = self.tile(st, "l0_carry", [128, 24, 3], F32)
            fw.op("pool", lambda E: E.memset(carry.t[:], 0.0), w=[carry.b])
            dtt = [self.tile(st, "l0_dt%d_%d" % (i // 3, i % 3), [128, 32], F32) for i in range(6)]
            cn = [self.tile(st, "l0_cn%d" % i, [128, 512], BF16) for i in range(2)]
            cst = [[self.tile(st, "l0_cst%d_%d" % (i, j), [128, 1], F32) for j in range(3)] for i in range(2)]
            cjunk = self.tile(st, "l0_cjunk", [128, 512], BF16)
            cqT = st.enter_context(self.nc.sbuf_tensor(self.nm("l0_cqT"), [128, 4, TT], BF16))
            ckvT = st.enter_context(self.nc.sbuf_tensor(self.nm("l0_ckvT"), [128, 4, TT], BF16))
            cqTb = [[self.key("l0_cqTb%d" % s)] for s in range(NSUB)]
            ckvTb = [[self.key("l0_ckvTb%d" % s)] for s in range(NSUB)]
            AfK = [self.tile(st, "l0_AfK%d" % i, [128, 512], F32) for i in range(2)]
            AfQ = [self.tile(st, "l0_AfQ%d" % i, [128, 512], F32) for i in range(2)]
            for a__ in AfK + AfQ:
                fw.op("pool", lambda E: E.memset(a__.t[:], 0.0), w=[a__.b])
            r1 = [self.tile(st, "l0_r1_%d" % i, [128, 512], F32) for i in range(2)]
            r2 = [self.tile(st, "l0_r2_%d" % i, [128, 512], F32) for i in range(2)]
            rb = [self.tile(st, "l0_rb%d" % i, [128, 512], BF16) for i in range(2)]
            CC = [self.tile(st, "l0_CC%d" % i, [128, 512], F32) for i in range(2)]
            SS = [self.tile(st, "l0_SS%d" % i, [128, 512], F32) for i in range(2)]
            cnt = {"pp": 0, "evb": 0, "evf": 0, "rope": 0}

            def nxt(nm, lst):
                i = cnt[nm]
                cnt[nm] += 1
                return lst[i % len(lst)]

            def store_bf(p_ap, p_b, dst, dname, M=128, view=None):
                e_ = nxt("evb", evb)
                i = cnt["evb"]
                if i % 2 == 0:
                    fw.op("act", lambda E: E.copy(out=e_.t[0:M, :], in_=p_ap), r=[p_b], w=[e_.b])
                else:
                    fw.op("dve", lambda E: E.tensor_copy(out=e_.t[0:M, :], in_=p_ap), r=[p_b], w=[e_.b])
                src = e_.t[0:M, :] if view is None else view(e_.t[0:M, :])
                fw.dma("act", dst, src, r=[e_.b], w=[self.db(*dname)], key=e_.b)

            def rope(pa, lo, cc, ss, dst, dname):
                i = cnt["rope"] % 2
                cnt["rope"] += 1
                pB = nxt("pp", pp)
                Af = (AfK if lo == 0 else AfQ)[i]
                sl = slice(lo, lo + 64)
                fw.op("act", lambda E: E.copy(out=Af.t[sl, :], in_=pa.t[sl, :]), r=[pa.b], w=[Af.b])
                fw.op("pe", lambda E: E.matmul(pB.t[:], lhsT=perm.t[:], rhs=Af.t[:], start=True, stop=True),
                      r=[perm.b, Af.b], w=[pB.b])
                fw.op("dve", lambda E: E.tensor_tensor(out=r1[i].t[sl, :], in0=Af.t[sl, :], in1=cc.t[sl, :], op=ALU.mult),
                      r=[Af.b, cc.b], w=[r1[i].b])
                fw.op("dve", lambda E: E.tensor_tensor(out=r2[i].t[sl, :], in0=pB.t[sl, :], in1=ss.t[sl, :], op=ALU.mult),
                      r=[pB.b, ss.b], w=[r2[i].b])
                fw.op("pool", lambda E: E.tensor_tensor(out=rb[i].t[sl, :], in0=r1[i].t[sl, :], in1=r2[i].t[sl, :], op=ALU.add),
                      r=[r1[i].b, r2[i].b], w=[rb[i].b])
                fw.dma("act", dst, rb[i].t[sl, :], r=[rb[i].b], w=[self.db(*dname)], key=rb[i].b)

            self.norm_T(nt, xin, xin_name, 0, gb, hTs[0], hTbs[0], pt)
            for tt in range(S // TT):
                tsl = slice(tt * TT, (tt + 1) * TT)
                self.pump(3)
                hT = hTs[tt % 2]
                hTb = hTbs[tt % 2]
                hT_all = [b for l in hTb for b in l]
                cc = CC[tt % 2]
                ss = SS[tt % 2]
                for lo_ in (0, 64):
                    fw.dma("sp", cc.t[lo_:lo_ + 64, :], P["rope_cc"][:, tsl], r=[self.db("in_rope_cc")], w=[cc.b])
                    fw.dma("sp", ss.t[lo_:lo_ + 64, :], P["rope_ss"][:, tsl], r=[self.db("in_rope_ss")], w=[ss.b])
                for ct in range(4):
                    wt = self.wload("wz", ct)
                    for s in range(NSUB):
                        p_ = nxt("pp", pp)
                        for kc in range(16):
                            fw.op("pe", lambda E: E.matmul(p_.t[:], lhsT=hT[:, kc, s * 128:(s + 1) * 128], rhs=wt.t[:, kc, :],
                                                           start=(kc == 0), stop=(kc == 15)), r=[wt.b] + hT_all, w=[p_.b])
                        e_ = nxt("evf", evf)
                        fw.op("act", lambda E: E.activation(out=e_.t[:], in_=p_.t[:], func=AF.Silu), r=[p_.b], w=[e_.b])
                        row = tt * NSUB + s
                        fw.dma("act", O["zs"][row * 128:(row + 1) * 128, ct * 512:(ct + 1) * 512], e_.t[:],
                               r=[e_.b], w=[self.db("zs", (row, ct))], key=e_.b)
                for ct in range(6):
                    wt = self.wload("wxbc", ct)
                    for j in range(4):
                        ch = ct * 4 + j
                        p_ = nxt("pp", pp)
                        for kc in range(16):
                            fw.op("pe", lambda E: E.matmul(p_.t[:], lhsT=wt.t[:, kc, j * 128:(j + 1) * 128], rhs=hT[:, kc, :],
                                                           start=(kc == 0), stop=(kc == 15)), r=[wt.b] + hT_all, w=[p_.b])
                        w_ = work[ch % 2]
                        a_ = acc[ch % 2]
                        fw.op("act", lambda E: E.copy(out=w_.t[:, 3:515], in_=p_.t[:]), r=[p_.b], w=[w_.b])
                        fw.op("pool", lambda E: E.tensor_copy(out=w_.t[:, 0:3], in_=carry.t[:, ch, :]), r=[carry.b], w=[w_.b])
                        fw.op("dve", lambda E: E.tensor_scalar(out=a_.t[:], in0=w_.t[:, 0:512], scalar1=cwT.t[:, ch:ch + 1],
                                                               scalar2=cwT.t[:, 96 + ch:97 + ch], op0=ALU.mult, op1=ALU.add),
                              r=[w_.b, cwT.b], w=[a_.b])
                        for i in range(1, 4):
                            fw.op("dve", lambda E: E.scalar_tensor_tensor(out=a_.t[:], in0=w_.t[:, i:i + 512],
                                                                          scalar=cwT.t[:, i * 24 + ch:i * 24 + ch + 1], in1=a_.t[:],
                                                                          op0=ALU.mult, op1=ALU.add),
                                  r=[w_.b, cwT.b, a_.b], w=[a_.b])
                        fw.op("pool", lambda E: E.tensor_copy(out=carry.t[:, ch, :], in_=w_.t[:, 512:515]), r=[w_.b], w=[carry.b])
                        e_ = nxt("evb", evb)
                        fw.op("act", lambda E: E.activation(out=e_.t[:], in_=a_.t[:], func=AF.Silu), r=[a_.b], w=[e_.b])
                        fw.dma("act", O["xbcT"][ch * 128:(ch + 1) * 128, tsl], e_.t[:], r=[e_.b], w=[self.db("xbcT", (ch, tt))], key=e_.b)
                if tt + 1 < S // TT:
                    self.norm_T(nt, xin, xin_name, tt + 1, gb, hTs[(tt + 1) % 2], hTbs[(tt + 1) % 2], pt)
                for s in range(NSUB):
                    row = tt * NSUB + s
                    p_ = nxt("pp", pp)
                    for kc in range(16):
                        fw.op("pe", lambda E: E.matmul(p_.t[:, 0:32], lhsT=hT[:, kc, s * 128:(s + 1) * 128], rhs=wsm.t[:, kc, 64:96],
                                                       start=(kc == 0), stop=(kc == 15)), r=[wsm.b] + hT_all, w=[p_.b])
                    d0, d1, d2 = dtt[(row % 2) * 3:(row % 2) * 3 + 3]
                    fw.op("dve", lambda E: E.tensor_tensor(out=d0.t[:], in0=p_.t[:, 0:32], in1=dtb.t[:], op=ALU.add), r=[p_.b, dtb.b], w=[d0.b])
                    fw.op("act", lambda E: E.activation(out=d0.t[:], in_=d0.t[:], func=AF.Exp), r=[d0.b], w=[d0.b])
                    fw.op("act", lambda E: E.activation(out=d1.t[:], in_=d0.t[:], func=AF.Ln, bias=1.0), r=[d0.b], w=[d1.b])
                    fw.op("dve", lambda E: E.tensor_tensor(out=d2.t[:], in0=d1.t[:], in1=ab.t[:], op=ALU.mult), r=[d1.b, ab.b], w=[d2.b])
                    fw.dma("act", O["dt"][row * 128:(row + 1) * 128, :], d1.t[:], r=[d1.b], w=[self.db("dt", row)], key=d1.b)
                    fw.dma("act", O["adt"][row * 128:(row + 1) * 128, :], d2.t[:], r=[d2.b], w=[self.db("adt", row)], key=d2.b)
                for (wn, gnb, cT, cTb) in (("wcq", gq, cqT, cqTb), ("wckv", gkv, ckvT, ckvTb)):
                    wt = self.wload(wn, 0)
                    for s in range(NSUB):
                        p_ = nxt("pp", pp)
                        for kc in range(16):
                            fw.op("pe", lambda E: E.matmul(p_.t[:], lhsT=hT[:, kc, s * 128:(s + 1) * 128], rhs=wt.t[:, kc, :],
                                                           start=(kc == 0), stop=(kc == 15)), r=[wt.b] + hT_all, w=[p_.b])
                        c_ = cn[s % 2]
                        cf_ = nxt("evf", evf)
                        fw.op("act", lambda E: E.copy(out=cf_.t[:], in_=p_.t[:]), r=[p_.b], w=[cf_.b])
                        self.rmsnorm(cf_.t[:], cf_.b, 512, gnb, c_.t[:], c_.b, cjunk, cst[s % 2])
                        self.transpose_to(c_.t, c_.b, 4, cT, cTb[s], 0, s * 128, pt, flip=s)
                cq_all = [l[0] for l in cqTb]
                ckv_all = [l[0] for l in ckvTb]
                pAk = nxt("pp", pp)
                for kc in range(16):
                    fw.op("pe", lambda E: E.matmul(pAk.t[0:96, :], lhsT=wsm.t[:, kc, 0:96], rhs=hT[:, kc, :],
                                                   start=(kc == 0), stop=(kc == 15)), r=[wsm.b] + hT_all, w=[pAk.b])
                pend = [(pAk, 0, O["kpT"][:, tsl], ("kpT", tt))]
                for t2 in range(2):
                    wt = self.wload("wuq", t2)
                    for hh in range(8):
                        h = t2 * 8 + hh
                        c0 = hh * 192
                        p_ = nxt("pp", pp)
                        for kc in range(4):
                            fw.op("pe", lambda E: E.matmul(p_.t[:], lhsT=wt.t[:, kc, c0:c0 + 128], rhs=cqT[:, kc, :],
                                                           start=(kc == 0), stop=(kc == 3)), r=[wt.b] + cq_all, w=[p_.b])
                        store_bf(p_.t[:], p_.b, O["qnT"][h, :, tsl], ("qnT", (h, tt)))
                        pA_ = nxt("pp", pp)
                        for kc in range(4):
                            fw.op("pe", lambda E: E.matmul(pA_.t[:], lhsT=wt.t[:, kc, c0 + 64:c0 + 192], rhs=cqT[:, kc, :],
                                                           start=(kc == 0), stop=(kc == 3)), r=[wt.b] + cq_all, w=[pA_.b])
                        pa0, lo0, dst0, dn0 = pend.pop(0)
                        rope(pa0, lo0, cc, ss, dst0, dn0)
                        pend.append((pA_, 64, O["qpT"][h, :, tsl], ("qpT", (h, tt))))
                pa0, lo0, dst0, dn0 = pend.pop(0)
                rope(pa0, lo0, cc, ss, dst0, dn0)
                for t2 in range(2):
                    wt = self.wload("wukv", t2)
                    for hh in range(8):
                        h = t2 * 8 + hh
                        c0 = hh * 256
                        p_ = nxt("pp", pp)
                        for kc in range(4):
                            fw.op("pe", lambda E: E.matmul(p_.t[:], lhsT=wt.t[:, kc, c0:c0 + 128], rhs=ckvT[:, kc, :],
                                                           start=(kc == 0), stop=(kc == 3)), r=[wt.b] + ckv_all, w=[p_.b])
                        store_bf(p_.t[:], p_.b, O["knT"][h, :, tsl], ("knT", (h, tt)))
                    for hg in range(2):
                        for s in range(NSUB):
                            row = tt * NSUB + s
                            p_ = nxt("pp", pp)
                            for kc in range(4):
                                rhs = wt.t[:, kc, :].rearrange("p (h two d) -> p h two d", two=2, d=128)[:, hg * 4:(hg + 1) * 4, 1, :]
                                fw.op("pe", lambda E: E.matmul(p_.t[:], lhsT=ckvT[:, kc, s * 128:(s + 1) * 128], rhs=rhs,
                                                               start=(kc == 0), stop=(kc == 3)), r=[wt.b] + ckv_all, w=[p_.b])
                            h0 = t2 * 8 + hg * 4
                            dst = O["vv"][h0:h0 + 4, row * 128:(row + 1) * 128, :].rearrange("h t d -> t h d")
                            store_bf(p_.t[:], p_.b, dst, ("vv0", (h0, row)), view=lambda a: a.rearrange("t (h d) -> t h d", h=4))
            self.barrier()

    def l0_ssd(self, P, O, ymT, ym_name):
        fw = self.fw
        S = self.S
        NCK = S // 128
        xbcT = O["xbcT"]
        with contextlib.ExitStack() as st:
            gn = self.bcast_row(st, "sd_gn", P["ssd_norm"], D)
            dsk = self.bcast_row(st, "sd_dsk", P["d_skip"], 32)
            def mkf(name, pattern, cm, cmp_):
                f = self.tile(st, "sd_" + name, [128, 128], F32)
                fw.op("pool", lambda E: E.memset(f.t[:], 1.0), w=[f.b])
                if pattern is not None:
                    fw.op("pool", lambda E: E.affine_select(out=f.t[:], in_=f.t[:], pattern=pattern, compare_op=cmp_, fill=0.0,
                                                            base=0, channel_multiplier=cm), r=[f.b], w=[f.b])
                return f
            U1 = mkf("U1", [[-1, 128]], 1, ALU.is_gt)
            U2 = mkf("U2", [[1, 128]], -1, ALU.is_ge)
            onesf = mkf("ones", None, 0, None)
            xT = [self.tile(st, "sd_xT%d" % i, [128, 16, 128], BF16) for i in range(2)]
            bcT = [self.tile(st, "sd_bcT%d" % i, [128, 8, 128], BF16) for i in range(2)]
            dtc = [self.tile(st, "sd_dt%d" % i, [128, 32], F32) for i in range(2)]
            adc = [self.tile(st, "sd_ad%d" % i, [128, 32], F32) for i in range(2)]
            zsc = [self.tile(st, "sd_zs%d" % i, [128, D], F32) for i in range(1)]
            xtok = [self.tile(st, "sd_xtok%d" % i, [128, D], BF16) for i in range(2)]
            btok = [self.tile(st, "sd_btok%d" % i, [128, 4, 128], BF16) for i in range(2)]
            xdt = [self.tile(st, "sd_xdt%d" % i, [128, D], BF16) for i in range(2)]
            xdte = [self.tile(st, "sd_xdte%d" % i, [128, D], BF16) for i in range(1)]
            Rm = self.tile(st, "sd_R", [128, 32, 128], F32)
            eseg = [self.tile(st, "sd_eseg%d" % i, [128, 512], F32) for i in range(2)]
            MT = [self.tile(st, "sd_MT%d" % i, [128, 32, 128], BF16) for i in range(1)]
            cbm = [self.tile(st, "sd_cbm%d" % i, [128, 4, 128], F32) for i in range(2)]
            sm = [[self.tile(st, "sd_sm%d_%d" % (i, j), [128, 32], F32) for j in range(4)] for i in range(2)]
            yA = [self.tile(st, "sd_yA%d" % i, [128, D], F32) for i in range(1)]
            yB = self.tile(st, "sd_yB", [128, D], F32)
            yo = [self.tile(st, "sd_yo%d" % i, [128, D], BF16) for i in range(1)]
            junk = self.tile(st, "sd_junk", [128, D], BF16)
            stt = [[self.tile(st, "sd_st%d_%d" % (i, j), [128, 1], F32) for j in range(3)] for i in range(2)]
            yT = st.enter_context(self.nc.sbuf_tensor(self.nm("sd_yT"), [128, 16, TT], BF16))
            yTb = [[self.key("sd_yTb%d_%d" % (s, h)) for h in range(2)] for s in range(NSUB)]
            prev = self.tile(st, "sd_prev", [128, 4, 512], F32)
            prevb = self.tile(st, "sd_prevb", [128, 4, 512], BF16)
            ptmp = self.tile(st, "sd_ptmp", [128, 512], F32)
            fw.op("pool", lambda E: E.memset(prev.t[:], 0.0), w=[prev.b])
            fw.op("pool", lambda E: E.memset(prevb.t[:], 0.0), w=[prevb.b])
            pt = self.tile(st, "sd_pt", [128, 8, 128], BF16, psum=True)
            Y = [self.tile(st, "sd_Y%d" % i, [128, 512], F32, psum=True) for i in range(4)]
            pCB = self.tile(st, "sd_pCB", [128, 4, 128], F32, psum=True)
            pSeg = self.tile(st, "sd_pSeg", [128, 512], F32, psum=True)
            pM = self.tile(st, "sd_pM", [128, 512], F32, psum=True)

            def bc_h(ap32, n):
                return ap32.unsqueeze(2).broadcast_to([128, ap32.shape[1], n])

            for c in range(NCK):
                i2 = c % 2
                self.pump(1)
                csl = slice(c * 128, (c + 1) * 128)
                x_, b_, d_, a_, z_ = xT[i2], bcT[i2], dtc[i2], adc[i2], zsc[0]
                fw.dma("sp", x_.t[:], xbcT[0:2048, csl].rearrange("(k p) t -> p k t", p=128), r=[self.db("xbcT_all")], w=[x_.b])
                fw.dma("sp", b_.t[:], xbcT[2048:3072, csl].rearrange("(k p) t -> p k t", p=128), r=[self.db("xbcT_all")], w=[b_.b])
                fw.dma("sp", d_.t[:], O["dt"][csl, :], r=[self.db("dt_all")], w=[d_.b])
                fw.dma("sp", a_.t[:], O["adt"][csl, :], r=[self.db("adt_all")], w=[a_.b])
                fw.dma("sp", z_.t[:], O["zs"][csl, :], r=[self.db("zs_all")], w=[z_.b])
                xt_, bt_, xd_, xde_ = xtok[i2], btok[i2], xdt[i2], xdte[0]
                for g0 in range(0, 16, 8):
                    for j in range(8):
                        fw.op("pe", lambda E: E.transpose(out=pt.t[:, j, :], in_=x_.t[:, g0 + j, :], identity=self.ident.t[:]),
                              r=[x_.b, self.ident.b], w=[pt.b])
                    if g0 == 0:
                        fw.op("act", lambda E: E.copy(out=xt_.t[:, 0:1024], in_=pt.t[:].rearrange("p a b -> p (a b)")), r=[pt.b], w=[xt_.b])
                    else:
                        fw.op("dve", lambda E: E.tensor_copy(out=xt_.t[:, 1024:2048], in_=pt.t[:].rearrange("p a b -> p (a b)")),
                              r=[pt.b], w=[xt_.b])
                for j in range(4):
                    fw.op("pe", lambda E: E.transpose(out=pt.t[:, j, :], in_=b_.t[:, j, :], identity=self.ident.t[:]),
                          r=[b_.b, self.ident.b], w=[pt.b])
                fw.op("act", lambda E: E.copy(out=bt_.t[:], in_=pt.t[:, 0:4, :]), r=[pt.b], w=[bt_.b])
                acs, tot, eacs, dte = sm[i2]
                fw.op("pe", lambda E: E.matmul(pSeg.t[:, 0:32], lhsT=U2.t[:], rhs=a_.t[:], start=True, stop=True),
                      r=[U2.b, a_.b], w=[pSeg.b])
                fw.op("pe", lambda E: E.matmul(pSeg.t[:, 32:64], lhsT=onesf.t[:], rhs=a_.t[:], start=True, stop=True),
                      r=[onesf.b, a_.b], w=[pSeg.b])
                fw.op("act", lambda E: E.activation(out=eacs.t[:], in_=pSeg.t[:, 0:32], func=AF.Exp), r=[pSeg.b], w=[eacs.b])
                fw.op("dve", lambda E: E.tensor_copy(out=acs.t[:], in_=pSeg.t[:, 0:32]), r=[pSeg.b], w=[acs.b])
                fw.op("dve", lambda E: E.tensor_tensor(out=dte.t[:], in0=pSeg.t[:, 32:64], in1=acs.t[:], op=ALU.subtract),
                      r=[pSeg.b, acs.b], w=[dte.b])
                fw.op("act", lambda E: E.activation(out=dte.t[:], in_=dte.t[:], func=AF.Exp), r=[dte.b], w=[dte.b])
                fw.op("act", lambda E: E.activation(out=tot.t[:], in_=pSeg.t[:, 32:64], func=AF.Exp), r=[pSeg.b], w=[tot.b])
                fw.op("dve", lambda E: E.tensor_tensor(out=xd_.t[:].rearrange("p (h d) -> p h d", d=64),
                                                       in0=xt_.t[:].rearrange("p (h d) -> p h d", d=64), in1=bc_h(d_.t[:], 64), op=ALU.mult),
                      r=[xt_.b, d_.b], w=[xd_.b])
                fw.op("pool", lambda E: E.tensor_tensor(out=xde_.t[:].rearrange("p (h d) -> p h d", d=64),
                                                        in0=xd_.t[:].rearrange("p (h d) -> p h d", d=64), in1=bc_h(dte.t[:], 64), op=ALU.mult),
                      r=[xd_.b, dte.b], w=[xde_.b])
                fw.op("dve", lambda E: E.tensor_tensor(out=Rm.t[:], in0=U2.t[:].unsqueeze(1).broadcast_to([128, 32, 128]),
                                                       in1=bc_h(a_.t[:], 128), op=ALU.mult), r=[U2.b, a_.b], w=[Rm.b])
                for g in range(4):
                    fw.op("pe", lambda E: E.matmul(pCB.t[:, g, :], lhsT=b_.t[:, g, :], rhs=b_.t[:, 4 + g, :], start=True, stop=True),
                          r=[b_.b], w=[pCB.b])
                cb_ = cbm[i2]
                fw.op("dve", lambda E: E.tensor_tensor(out=cb_.t[:], in0=pCB.t[:], in1=U2.t[:].unsqueeze(1).broadcast_to([128, 4, 128]),
                                                       op=ALU.mult), r=[pCB.b, U2.b], w=[cb_.b])
                mt_ = MT[0]
                for q4 in range(8):
                    g = q4 // 2
                    es_ = eseg[q4 % 2]
                    psg = pSeg if q4 % 2 == 0 else Y[0]
                    fw.op("pe", lambda E: E.matmul(psg.t[:], lhsT=U1.t[:], rhs=Rm.t[:, q4 * 4:(q4 + 1) * 4, :], start=True, stop=True),
                          r=[U1.b, Rm.b], w=[psg.b])
                    fw.op("act", lambda E: E.activation(out=es_.t[:], in_=psg.t[:], func=AF.Exp), r=[psg.b], w=[es_.b])
                    fw.op("dve", lambda E: E.tensor_tensor(out=mt_.t[:, q4 * 4:(q4 + 1) * 4, :],
                                                           in0=es_.t[:].rearrange("p (h l) -> p h l", l=128),
                                                           in1=cb_.t[:, g:g + 1, :].broadcast_to([128, 4, 128]), op=ALU.mult),
                          r=[es_.b, cb_.b], w=[mt_.b])
                for h in range(32):
                    Yb = Y[h // 8]
                    fw.op("pe", lambda E: E.matmul(Yb.t[:, (h % 8) * 64:(h % 8 + 1) * 64], lhsT=mt_.t[:, h, :], rhs=xd_.t[:, h * 64:(h + 1) * 64],
                                                   start=True, stop=True), r=[mt_.b, xd_.b], w=[Yb.b])
                ya_ = yA[0]
                fw.op("pool", lambda E: E.tensor_tensor(out=yB.t[:].rearrange("p (h d) -> p h d", d=64),
                                                        in0=xt_.t[:].rearrange("p (h d) -> p h d", d=64), in1=bc_h(dsk.t[:], 64), op=ALU.mult),
                      r=[xt_.b, dsk.b], w=[yB.b])
                for g in range(4):
                    gs = slice(g * 512, (g + 1) * 512)
                    fw.op("pe", lambda E: E.matmul(pM.t[:], lhsT=b_.t[:, 4 + g, :], rhs=prevb.t[:, g, :], start=True, stop=True),
                          r=[b_.b, prevb.b], w=[pM.b])
                    fw.op("dve", lambda E: E.tensor_tensor(out=ya_.t[:, gs].rearrange("p (h d) -> p h d", d=64),
                                                           in0=pM.t[:].rearrange("p (h d) -> p h d", d=64),
                                                           in1=bc_h(eacs.t[:, g * 8:(g + 1) * 8], 64), op=ALU.mult),
                          r=[pM.b, eacs.b], w=[ya_.b])
                    fw.op("dve", lambda E: E.tensor_tensor(out=ya_.t[:, gs], in0=Y[g].t[:], in1=ya_.t[:, gs], op=ALU.add),
                          r=[Y[g].b, ya_.b], w=[ya_.b])
                    fw.op("pe", lambda E: E.matmul(pSeg.t[:], lhsT=bt_.t[:, g, :], rhs=xde_.t[:, gs], start=True, stop=True),
                          r=[bt_.b, xde_.b], w=[pSeg.b])
                    fw.op("dve", lambda E: E.tensor_tensor(out=ptmp.t[:].rearrange("p (h d) -> p h d", d=64),
                                                           in0=prev.t[:, g, :].rearrange("p (h d) -> p h d", d=64),
                                                           in1=bc_h(tot.t[:, g * 8:(g + 1) * 8], 64), op=ALU.mult),
                          r=[prev.b, tot.b], w=[ptmp.b])
                    fw.op("dve", lambda E: E.tensor_tensor(out=prev.t[:, g, :], in0=pSeg.t[:], in1=ptmp.t[:], op=ALU.add),
                          r=[pSeg.b, ptmp.b], w=[prev.b])
                    fw.op("act", lambda E: E.copy(out=prevb.t[:, g, :], in_=prev.t[:, g, :]), r=[prev.b], w=[prevb.b])
                fw.op("pool", lambda E: E.tensor_tensor(out=ya_.t[:], in0=ya_.t[:], in1=yB.t[:], op=ALU.add), r=[ya_.b, yB.b], w=[ya_.b])
                fw.op("dve", lambda E: E.tensor_tensor(out=ya_.t[:], in0=ya_.t[:], in1=z_.t[:], op=ALU.mult), r=[ya_.b, z_.b], w=[ya_.b])
                yo_ = yo[0]
                self.rmsnorm(ya_.t[:], ya_.b, D, gn, yo_.t[:], yo_.b, junk, stt[i2])
                s4 = c % NSUB
                self.transpose_to(yo_.t, yo_.b, 16, yT, yTb[s4], 0, s4 * 128, pt, flip=c)
                if s4 == NSUB - 1:
                    tt = c // NSUB
                    allb = [b for l in yTb for b in l]
                    fw.dma("act", ymT[0:2048, tt * TT:(tt + 1) * TT].rearrange("(k p) t -> p k t", p=128), yT[:],
                           r=allb, w=[self.db(ym_name, ("ssd", tt))], key=allb[0])
            self.barrier()


IN_OFF = (0, 2048, 5120, 5152, 5664, 6176, 6240)


def model_decl(S):
    return {
        "x": (S, D), "mix_norm": (2, D), "ffn_norm": (2, D), "w_in": (1, D, 6240), "conv_w": (1, 4, 3072),
        "conv_b": (1, 3072), "dt_bias": (1, 32), "a_log": (1, 32), "d_skip": (1, 32), "ssd_norm": (1, D),
        "q_norm": (1, 512), "kv_norm": (1, 512), "w_uq": (1, 512, 3072), "w_ukv": (1, 512, 4096),
        "w_out_even": (1, 4096, D), "w_qkv": (1, D, 3 * D), "w_out_odd": (1, D, D),
        "w_gate": (2, D, FH), "w_up": (2, D, FH), "w_down": (2, FH, D), "final_norm": (1, D),
        "rope_cc": (64, S), "rope_ss": (64, S),
    }


def prep_l0_weights(bld):
    I = bld.inp
    w_in = I["w_in"][0]
    bld.prep_w("wz", w_in[:, 0:2048])
    bld.prep_w("wxbc", w_in[:, 2048:5120])
    bld.prep_w("wcq", w_in[:, 5152:5664])
    bld.prep_w("wckv", w_in[:, 5664:6176])
    ws = bld.dram("ws_wsm", [1, 1, 128, 16, 96], BF16)
    tb = Buf("w_wsm")
    k_ = bld.cast_dma(ws[0, 0, :, :, 64:96], w_in[:, 5120:5152].rearrange("(kc p) n -> p kc n", p=128), bld.db("in_w_in"), tb)
    bld.cast_dma(ws[0, 0, :, :, 0:64], w_in[:, 6176:6240].rearrange("(kc p) n -> p kc n", p=128), bld.db("in_w_in"), tb, same_key=k_)
    bld.wts["wsm"] = (ws, {(0, 0): tb}, 16, 1, 1)
    bld.prep_w("wuq", I["w_uq"][0], cw=1536)
    bld.prep_w("wukv", I["w_ukv"][0], cw=2048)
    bld.prep_w("wo0", I["w_out_even"][0])


def l0_scratch(bld):
    S = bld.S
    O = {}
    O["zs"] = bld.dram("sc_zs", [S, D], F32)
    O["xbcT"] = bld.dram("sc_xbcT", [3072, S], BF16)
    O["dt"] = bld.dram("sc_dt", [S, 32], F32)
    O["adt"] = bld.dram("sc_adt", [S, 32], F32)
    O["kpT"] = bld.dram("sc_kpT", [64, S], BF16)
    O["qnT"] = bld.dram("sc_qnT", [16, 128, S], BF16)
    O["qpT"] = bld.dram("sc_qpT", [16, 64, S], BF16)
    O["knT"] = bld.dram("sc_knT", [16, 128, S], BF16)
    O["vv"] = bld.dram("sc_vv0", [16, S, 128], BF16)
    return O


def l0_params(bld):
    I = bld.inp
    return {"mix_norm": I["mix_norm"][0, :], "q_norm": I["q_norm"][0, :], "kv_norm": I["kv_norm"][0, :], "dt_bias": I["dt_bias"][0, :],
            "a_log": I["a_log"][0, :], "conv_w": I["conv_w"][0], "conv_b": I["conv_b"][0, :], "rope_cc": I["rope_cc"], "rope_ss": I["rope_ss"],
            "ssd_norm": I["ssd_norm"][0, :], "d_skip": I["d_skip"][0, :]}


def layer0_mixer(bld, xin, xin_name, xout, xout_name, ym0):
    O = l0_scratch(bld)
    P = l0_params(bld)
    bld.l0_inproj(xin, xin_name, P, O)
    bld.l0_ssd(P, O, ym0, "ym0")
    bld.attention("mla", 16, O["qnT"], O["knT"], O["vv"], ym0, "ym0", 2048, 192 ** -0.5, qpT=O["qpT"], kpT=O["kpT"])
    bld.out_proj("wo0", 32, ym0, "ym0", xin, xin_name, xout, xout_name)


def rope_tables_np(S):
    half = 32
    inv_freq = (np.float32(10000.0) ** (-np.arange(half, dtype=np.float32) / np.float32(half))).astype(np.float32)
    ang = (np.arange(S, dtype=np.float32)[:, None] * inv_freq[None, :]).astype(np.float32)
    cos = np.cos(ang).astype(np.float32).T
    sin = np.sin(ang).astype(np.float32).T
    return np.ascontiguousarray(np.concatenate([cos, cos], 0)), np.ascontiguousarray(np.concatenate([sin, sin], 0))


def prep_rest_weights(bld):
    I = bld.inp
    bld.prep_w("wg0", I["w_gate"][0], defer=True)
    bld.prep_w("wu0", I["w_up"][0], defer=True)
    bld.prep_w("wd0", I["w_down"][0], kcg_max=11, defer=True)
    bld.prep_w("wqkv", I["w_qkv"][0], defer=True)
    bld.prep_w("wo1", I["w_out_odd"][0], defer=True)
    bld.prep_w("wg1", I["w_gate"][1], defer=True)
    bld.prep_w("wu1", I["w_up"][1], defer=True)
    bld.prep_w("wd1", I["w_down"][1], kcg_max=11, defer=True)


def build_model(S):
    bld = Builder(S, model_decl(S))
    I = bld.inp
    with bld.root as st:
        out = bld.dram("out", [S, D], F32, kind="ExternalOutput")
        x1 = bld.dram("sc_x1", [S, D], F32)
        x2 = bld.dram("sc_x2", [S, D], F32)
        x3 = bld.dram("sc_x3", [S, D], F32)
        x4 = bld.dram("sc_x4", [S, D], F32)
        ym0 = bld.dram("sc_ym0", [4096, S], BF16)
        ym1 = bld.dram("sc_ym1", [2048, S], BF16)
        qT = bld.dram("sc_qT", [16, 128, S], BF16)
        kT = bld.dram("sc_kT", [16, 128, S], BF16)
        vv = bld.dram("sc_vv1", [16, S, 128], BF16)
        bld.setup_consts(st)
        prep_l0_weights(bld)
        prep_rest_weights(bld)
        O = l0_scratch(bld)
        P = l0_params(bld)
        bld.l0_inproj(I["x"], "x", P, O)
        bld.l0_ssd(P, O, ym0, "ym0")
        bld.attention("mla", 16, O["qnT"], O["knT"], O["vv"], ym0, "ym0", 2048, 192 ** -0.5, qpT=O["qpT"], kpT=O["kpT"])
        bld.flush_casts()
        bld.out_proj("wo0", 32, ym0, "ym0", I["x"], "x", x1, "x1")
        bld.ffn(0, x1, "x1", x2, "x2", I["ffn_norm"][0, :])
        bld.sb_qkv(x2, "x2", I["mix_norm"][1, :], qT, kT, vv, 128 ** -0.5)
        bld.attention("sb", 16, qT, kT, vv, ym1, "ym1", 0, 1.0)
        bld.out_proj("wo1", 16, ym1, "ym1", x2, "x2", x3, "x3")
        bld.ffn(1, x3, "x3", x4, "x4", I["ffn_norm"][1, :])
        bld.final_norm(x4, "x4", out, "out", I["final_norm"][0, :])
    return bld


N_CORES = 4


def kernel(**inputs):
    x = np.asarray(inputs["x"], dtype=np.float32)
    B, S, _ = x.shape
    cc, ss = rope_tables_np(S)
    shared = {}
    for k, shp in model_decl(S).items():
        if k in ("x", "rope_cc", "rope_ss"):
            continue
        shared[k] = np.ascontiguousarray(np.asarray(inputs[k], dtype=np.float32).reshape(shp))
    shared["rope_cc"] = cc
    shared["rope_ss"] = ss
    bld = build_model(S)
    in_maps = []
    for b in range(B):
        m = dict(shared)
        m["x"] = np.ascontiguousarray(x[b])
        in_maps.append(m)
    res = run_bass_kernel_spmd(bld.nc, in_maps, core_ids=list(range(B)))
    return np.stack([np.asarray(res.results[b]["out"], dtype=np.float32) for b in range(B)], axis=0)
```
